# Optimizing a Trainium2 kernel written in Bass

```python
import jax, jax.numpy as jnp
from jax import lax
import numpy as np

D_MODEL = 1024
BATCH = 4
SEQ = 8192
DEPTH = 1

N_META = 16
HEAD_DIM = 64
ATTN_HEADS = (D_MODEL // 2) // HEAD_DIM
KV_HEADS = 2
GQA_GROUP = ATTN_HEADS // KV_HEADS
ATTN_WIDTH = ATTN_HEADS * HEAD_DIM
KV_WIDTH = KV_HEADS * HEAD_DIM
LRU_WIDTH = D_MODEL // 2
LRU_BLOCKS = 8
LRU_BLOCK = LRU_WIDTH // LRU_BLOCKS
LRU_C = 8.0
CONV_WIDTH = 4
WINDOW = 128
BLOCK = 128
PAD = BLOCK - N_META
MIX_WIDTH = ATTN_WIDTH + LRU_WIDTH
IN_WIDTH = ATTN_WIDTH + 2 * KV_WIDTH + 2 * LRU_WIDTH
D_FF = 4 * D_MODEL
EPS = 1e-6
NEG = -1e30

kernel_name = "hymba_griffin_swa_sink_hybrid"


def rmsnorm(x, g):
    xf = x.astype(jnp.float32)
    y = xf * lax.rsqrt(jnp.mean(xf * xf, axis=-1, keepdims=True) + EPS)
    return (y * g.astype(jnp.float32)).astype(x.dtype)


def causal_depthwise_conv(x, w, b):
    c = x.shape[-1]
    y = lax.conv_general_dilated(
        x, w[:, None, :].astype(x.dtype), window_strides=(1,),
        padding=[(CONV_WIDTH - 1, 0)],
        dimension_numbers=("NWC", "WIO", "NWC"), feature_group_count=c)
    return y + b.astype(x.dtype)


def rg_lru(x, w_a, b_a, w_x, b_x, lam):
    bsz, t, _ = x.shape
    xb = x.reshape(bsz, t, LRU_BLOCKS, LRU_BLOCK)
    gate_r = jnp.einsum("btnc,ncd->btnd", xb, w_a).reshape(bsz, t, LRU_WIDTH) + b_a
    gate_i = jnp.einsum("btnc,ncd->btnd", xb, w_x).reshape(bsz, t, LRU_WIDTH) + b_x
    r = jax.nn.sigmoid(gate_r.astype(jnp.float32))
    i = jax.nn.sigmoid(gate_i.astype(jnp.float32))
    log_a = -LRU_C * r * jax.nn.softplus(-lam.astype(jnp.float32))
    a = jnp.exp(log_a)
    mult = jnp.sqrt(-jnp.expm1(2.0 * log_a))
    u = mult * (i * x.astype(jnp.float32))

    def combine(left, right):
        a_l, b_l = left
        a_r, b_r = right
        return a_l * a_r, a_r * b_l + b_r

    _, h = lax.associative_scan(combine, (a, u), axis=1)
    return h


def sliding_window_attention_with_sinks(q, k, v, sinks):
    bsz, t, _, _ = q.shape
    pad_cfg = ((0, 0), (PAD, 0), (0, 0), (0, 0))
    qp, kp, vp = jnp.pad(q, pad_cfg), jnp.pad(k, pad_cfg), jnp.pad(v, pad_cfg)
    tp = t + PAD
    nb = tp // BLOCK
    qb = qp.reshape(bsz, nb, BLOCK, KV_HEADS, GQA_GROUP, HEAD_DIM)
    kb = kp.reshape(bsz, nb, BLOCK, KV_HEADS, HEAD_DIM)
    vb = vp.reshape(bsz, nb, BLOCK, KV_HEADS, HEAD_DIM)
    blk_pad = ((0, 0), (1, 0), (0, 0), (0, 0), (0, 0))
    kk = jnp.concatenate([jnp.pad(kb, blk_pad)[:, :-1], kb], axis=2)
    vv = jnp.concatenate([jnp.pad(vb, blk_pad)[:, :-1], vb], axis=2)
    scale = HEAD_DIM ** -0.5
    s = jnp.einsum("bnqkgd,bnskd->bnkgqs", qb, kk,
                   preferred_element_type=jnp.float32) * scale
    blk = jnp.arange(nb)[:, None] * BLOCK
    qpos = blk + jnp.arange(BLOCK)[None, :]
    kpos = blk - BLOCK + jnp.arange(2 * BLOCK)[None, :]
    diff = qpos[:, :, None] - kpos[:, None, :]
    valid = (diff >= 0) & (diff < WINDOW) & (kpos[:, None, :] >= PAD)
    s = jnp.where(valid[None, :, None, None], s, NEG)
    sink = sinks.astype(jnp.float32).reshape(1, 1, KV_HEADS, GQA_GROUP, 1, 1)
    m = jnp.maximum(jnp.max(s, axis=-1, keepdims=True), sink)
    p = jnp.exp(s - m)
    denom = jnp.sum(p, axis=-1, keepdims=True) + jnp.exp(sink - m)
    o = jnp.einsum("bnkgqs,bnskd->bnqkgd", (p / denom).astype(v.dtype), vv)
    return o.reshape(bsz, tp, ATTN_WIDTH)[:, PAD:]


def setup_inputs(seed: int = 0) -> dict:
    key = jax.random.key(seed)
    ks = jax.random.split(key, 20)
    f32 = jnp.float32

    def nrm(k, shape, scale):
        return jax.random.normal(k, shape, f32) * scale

    u = jax.random.uniform(ks[10], (DEPTH, LRU_WIDTH), f32, 0.9, 0.999)
    a0 = u ** (1.0 / LRU_C)
    lru_lambda = jnp.log(a0) - jnp.log1p(-a0)
    return {
        "x": nrm(ks[0], (BATCH, SEQ, D_MODEL), 1.0),
        "meta_tokens": nrm(ks[1], (N_META, D_MODEL), 1.0),
        "g_pre_mix": 1.0 + nrm(ks[2], (DEPTH, D_MODEL), 0.05),
        "w_in": nrm(ks[3], (DEPTH, D_MODEL, IN_WIDTH), D_MODEL ** -0.5),
        "conv_w": nrm(ks[4], (DEPTH, CONV_WIDTH, LRU_WIDTH), CONV_WIDTH ** -0.5),
        "conv_b": nrm(ks[5], (DEPTH, LRU_WIDTH), 0.01),
        "w_a": nrm(ks[6], (DEPTH, LRU_BLOCKS, LRU_BLOCK, LRU_BLOCK), LRU_BLOCK ** -0.5),
        "b_a": nrm(ks[7], (DEPTH, LRU_WIDTH), 0.01),
        "w_x": nrm(ks[8], (DEPTH, LRU_BLOCKS, LRU_BLOCK, LRU_BLOCK), LRU_BLOCK ** -0.5),
        "b_x": nrm(ks[9], (DEPTH, LRU_WIDTH), 0.01),
        "lru_lambda": lru_lambda,
        "attn_sinks": nrm(ks[11], (DEPTH, ATTN_HEADS), 0.5),
        "w_out": nrm(ks[12], (DEPTH, MIX_WIDTH, D_MODEL), MIX_WIDTH ** -0.5),
        "g_post_mix": 1.0 + nrm(ks[13], (DEPTH, D_MODEL), 0.05),
        "g_pre_ffn": 1.0 + nrm(ks[14], (DEPTH, D_MODEL), 0.05),
        "w_ff1": nrm(ks[15], (DEPTH, D_MODEL, D_FF), D_MODEL ** -0.5),
        "w_ff2": nrm(ks[16], (DEPTH, D_FF, D_MODEL), D_FF ** -0.5),
        "g_post_ffn": 1.0 + nrm(ks[17], (DEPTH, D_MODEL), 0.05),
    }


def reference(x, meta_tokens, g_pre_mix, w_in, conv_w, conv_b, w_a, b_a, w_x, b_x,
              lru_lambda, attn_sinks, w_out, g_post_mix, g_pre_ffn, w_ff1, w_ff2,
              g_post_ffn):
    bsz = x.shape[0]
    meta = jnp.broadcast_to(meta_tokens[None].astype(x.dtype), (bsz, N_META, D_MODEL))
    h = jnp.concatenate([meta, x], axis=1)
    t = h.shape[1]
    split_at = [ATTN_WIDTH, ATTN_WIDTH + KV_WIDTH, ATTN_WIDTH + 2 * KV_WIDTH,
                ATTN_WIDTH + 2 * KV_WIDTH + LRU_WIDTH]
    for l in range(DEPTH):
        u = rmsnorm(h, g_pre_mix[l])
        z = u @ w_in[l]
        q, k, v, xr, yr = jnp.split(z, split_at, axis=-1)
        attn = sliding_window_attention_with_sinks(
            q.reshape(bsz, t, ATTN_HEADS, HEAD_DIM),
            k.reshape(bsz, t, KV_HEADS, HEAD_DIM),
            v.reshape(bsz, t, KV_HEADS, HEAD_DIM),
            attn_sinks[l])
        xr = causal_depthwise_conv(xr, conv_w[l], conv_b[l])
        hr = rg_lru(xr, w_a[l], b_a[l], w_x[l], b_x[l], lru_lambda[l])
        rec = (jax.nn.gelu(yr.astype(jnp.float32)) * hr).astype(h.dtype)
        mix = jnp.concatenate([attn.astype(h.dtype), rec], axis=-1) @ w_out[l]
        h = h + rmsnorm(mix, g_post_mix[l])
        u = rmsnorm(h, g_pre_ffn[l])
        f = jnp.square(jax.nn.relu(u @ w_ff1[l])) @ w_ff2[l]
        h = h + rmsnorm(f, g_post_ffn[l])
    return h[:, N_META:]
```

```python
from contextlib import ExitStack

import numpy as np
import concourse.bass as bass
import concourse.mybir as mybir
from concourse.bass_utils import run_bass_kernel_spmd

F32 = mybir.dt.float32
BF16 = mybir.dt.bfloat16
AF = mybir.ActivationFunctionType
ALU = mybir.AluOpType

D = 1024
NT_MAIN = 32
NPRE = 4224
G = 512
EPS = 1e-6
GELU_C = 0.7978845608028654


class Buf:
    __slots__ = ("name", "last_write", "readers")

    def __init__(self, name):
        self.name = name
        self.last_write = None
        self.readers = {}


class Op:
    __slots__ = ("eng", "fn", "deps", "signals", "key", "tick", "is_dma", "idx")

    def __init__(self, eng, fn, key, is_dma):
        self.eng = eng
        self.fn = fn
        self.key = key
        self.is_dma = is_dma
        self.deps = []
        self.signals = False
        self.tick = 0


ENGS = ("sp", "act", "dve", "pool", "pe")


class Sched:
    def __init__(self):
        self.queues = {e: [] for e in ENGS}
        self.nops = 0

    def add(self, eng, fn, reads=(), writes=(), dma_key=None, extra_deps=()):
        is_dma = dma_key is not None
        key = dma_key if is_dma else eng
        op = Op(eng, fn, key, is_dma)
        op.idx = self.nops
        self.nops += 1
        deps = {}
        for b in reads:
            w = b.last_write
            if w is not None:
                deps[id(w)] = w
        for b in writes:
            w = b.last_write
            if w is not None:
                deps[id(w)] = w
            for r in b.readers.values():
                deps[id(r)] = r
        for d in extra_deps:
            deps[id(d)] = d
        out = []
        for d in deps.values():
            if d is op:
                continue
            if d.key == "pe" and key == "pe":
                continue
            out.append(d)
            d.signals = True
        op.deps = out
        for b in reads:
            b.readers[key] = op
        for b in writes:
            b.last_write = op
            b.readers = {}
        self.queues[eng].append(op)
        return op

    def emit(self, nc):
        counts = {}
        keys = []
        allops = []
        for e in ENGS:
            allops.extend(self.queues[e])
        allops.sort(key=lambda o: o.idx)
        for op in allops:
            if op.key not in counts:
                counts[op.key] = 0
                keys.append(op.key)
            if op.signals:
                counts[op.key] += 16 if op.is_dma else 1
            op.tick = counts[op.key]
        with ExitStack() as es:
            sems = {k: es.enter_context(nc.semaphore("s_" + k)) for k in keys}
            with nc.Block() as block:
                def run(engname, eng):
                    waited = {}
                    for op in self.queues[engname]:
                        need = {}
                        for d in op.deps:
                            if d.tick > need.get(d.key, 0):
                                need[d.key] = d.tick
                        for k, v in need.items():
                            if waited.get(k, 0) < v:
                                eng.wait_ge(sems[k], v)
                                waited[k] = v
                        if op.fn is not None:
                            ins = op.fn(eng)
                            if op.signals:
                                ins.then_inc(sems[op.key], 16 if op.is_dma else 1)

                @block.sync
                def _(eng):
                    run("sp", eng)

                @block.scalar
                def _(eng):
                    run("act", eng)

                @block.vector
                def _(eng):
                    run("dve", eng)

                @block.gpsimd
                def _(eng):
                    run("pool", eng)

                @block.tensor
                def _(eng):
                    run("pe", eng)


def build_program(n_main_groups=8, n_pre_groups=9, pipelined=True):
    nc = bass.Bass("TRN2", target_bir_lowering=False)
    S = Sched()

    def din(name, shape):
        return nc.dram_tensor(name, list(shape), F32, kind="ExternalInput")

    xp = din("xp", [NPRE, D])
    xm = din("xm", [NT_MAIN * 128, D])
    vrow = din("vrow", [1, NPRE])
    vcol = din("vcol", [128, 1])
    w_in = din("w_in", [D, 1792])
    w_out = din("w_out", [D, D])
    w_ff1 = din("w_ff1", [D, 4096])
    w_ff2 = din("w_ff2", [4096, D])
    wa_bd = din("wa_bd", [4, 128, 128])
    wx_bd = din("wx_bd", [4, 128, 128])
    cw_d = din("cw", [128, 16])
    cb_d = din("cb", [128, 4])
    ba_d = din("ba", [128, 4])
    bx_d = din("bx", [128, 4])
    lam_d = din("lam", [128, 4])
    sink_d = din("sinks", [1, 8])
    g1c_d = din("g1c", [128, 8])
    g2_d = din("g2", [1, D])
    g3_d = din("g3", [1, D])
    g4_d = din("g4", [1, D])
    out_d = nc.dram_tensor("out", [NT_MAIN * 128, D], F32, kind="ExternalOutput")

    es = ExitStack()

    def sb(name, shape, dt=F32):
        return es.enter_context(nc.sbuf_tensor("sb_" + name, list(shape), dt))

    def B(name):
        return Buf(name)

    Win = sb("Win", [128, 8, 1792], BF16)
    Wa = sb("Wa", [128, 4, 128], BF16)
    Wx = sb("Wx", [128, 4, 128], BF16)
    gt2 = sb("gt2", [128, D])
    gt3 = sb("gt3", [128, D])
    gt4 = sb("gt4", [128, D])
    wst = [sb(f"wst{i}", [128, 4096], BF16) for i in range(2)]
    xt = [sb(f"xt{i}", [128, D]) for i in range(2)]
    h1 = sb("h1", [128, 4, D])
    ub = [sb(f"ub{i}", [128, D], BF16) for i in range(3)]
    uT = sb("uT", [128, 8, G], BF16)
    u2T = sb("u2T", [128, 8, G], BF16)
    qT = sb("qT", [128, 4, G], BF16)
    KA = sb("KA", [128, 640], BF16)
    KB = sb("KB", [128, 640], BF16)
    Vaug = sb("Vaug", [128, 5, 2, 65], BF16)
    PT = [sb(f"PT{i}", [128, 512], BF16) for i in range(2)]
    maskPT = sb("maskPT", [128, 512], BF16)
    atok = [sb(f"atok{i}", [128, 512], BF16) for i in range(2)]
    attnT = sb("attnT", [128, 4, G], BF16)
    recT = sb("recT", [128, 4, G], BF16)
    xrs = [sb(f"xrs{i}", [128, G + 3]) for i in range(2)]
    TT = [[sb(f"T{p}_{i}", [128, G]) for i in range(5)] for p in range(2)]
    xcbs = [sb(f"xcb{p}", [128, G], BF16) for p in range(2)]
    hist = sb("hist", [128, 4, 3])
    hstate = sb("hstate", [128, 4])
    ftok = sb("ftok", [128, 4, D])
    tmp2 = sb("tmp2", [128, D])
    f1T = sb("f1T", [128, 32, G], BF16)
    fTc = tmp2[:, 0:G]
    vm = tmp2[:, G:2 * G]
    cw = sb("cw", [128, 16])
    cb = sb("cb", [128, 4])
    nba = sb("nba", [128, 4])
    nbx = sb("nbx", [128, 4])
    cneg = sb("cneg", [128, 4])
    c2 = sb("c2", [128, 4])
    nsink = sb("nsink", [128, 8])
    g1c = sb("g1c", [128, 8])
    vcol_s = sb("vcol_s", [128, 1])
    ident_f = sb("ident_f", [128, 128])
    ident = sb("ident", [128, 128], BF16)
    ones_f = sb("ones_f", [128, 128])
    mtmp = sb("mtmp", [128, 128])
    st_ms = sb("st_ms", [128, 8])
    st_rs = sb("st_rs", [128, 8])
    den = sb("den", [128, 8])
    ps = es.enter_context(nc.psum_tensor("ps", [128, 8, 512], F32))
    psb = ps.bitcast(BF16)

    bWin = [B(f"Win{k}") for k in range(8)]
    bWa, bWx = B("Wa"), B("Wx")
    bgt2, bgt3, bgt4 = B("gt2"), B("gt3"), B("gt4")
    bwst = [B("wst0"), B("wst1")]
    bxt = [B("xt0"), B("xt1")]
    bh1 = [B(f"h1_{t}") for t in range(4)]
    bub = [B(f"ub{i}") for i in range(3)]
    buT = [B(f"uT{t}") for t in range(4)]
    bu2T = [B(f"u2T{t}") for t in range(4)]
    bqT = [B(f"qT{c}") for c in range(4)]
    bK = [B(f"K{b}") for b in range(5)]
    bV = [B(f"V{b}") for b in range(5)]
    bPT = [B("PT0"), B("PT1")]
    bmask = B("maskPT")
    batok = [B("atok0"), B("atok1")]
    battnT = [B(f"attnT{t}") for t in range(4)]
    brecT = [B(f"recT{j}") for j in range(4)]
    bxrs = [B("xrs0"), B("xrs1")]
    bTT = [[B(f"T{p}_{i}") for i in range(5)] for p in range(2)]
    bxcbs = [B("xcb0"), B("xcb1")]
    bhist = [B(f"hist{j}") for j in range(4)]
    bhst = [B(f"hst{j}") for j in range(4)]
    bftok = [B(f"ftok{t}") for t in range(4)]
    btmp2 = B("tmp2")
    bf1T = [B(f"f1T{m}") for m in range(32)]
    bfTc = btmp2
    bvm = btmp2
    bvmh = [B("vmh0"), B("vmh1")]
    bconst = B("const")
    bident = B("ident")
    bst = [B(f"st{i}") for i in range(8)]
    bden = B("den")
    bps = [B(f"ps{i}") for i in range(8)]

    class Rot:
        def __init__(self, ids):
            self.ids = list(ids)
            self.i = 0

        def next(self):
            v = self.ids[self.i % len(self.ids)]
            self.i += 1
            return v

    bankF = Rot([0, 1, 2])
    bankB = Rot([5, 6, 7])
    statF = Rot([0, 1, 2, 3])
    statB = Rot([4, 5, 6, 7])
    out_ops = []
    cost = {"A": 0.0, "B": 0.0}
    state = {"cur": "A"}

    def act(fn, reads, writes):
        return S.add("act", fn, reads, writes)

    def dve(fn, reads, writes):
        return S.add("dve", fn, reads, writes)

    def pool(fn, reads, writes):
        return S.add("pool", fn, reads, writes)

    def pe(fn, reads, writes, c=512):
        cost[state["cur"]] += max(c, 64)
        return S.add("pe", fn, reads, writes)

    def setup():
        sp = lambda fn, w, key: S.add("sp", fn, (), w, dma_key=key)
        sp(lambda e: e.dma_start(out=cw[:], in_=cw_d.ap()), [bconst], "d_c0")
        sp(lambda e: e.dma_start(out=cb[:], in_=cb_d.ap()), [bconst], "d_c0")
        sp(lambda e: e.dma_start(out=nba[:], in_=ba_d.ap()), [bconst], "d_c0")
        sp(lambda e: e.dma_start(out=nbx[:], in_=bx_d.ap()), [bconst], "d_c0")
        sp(lambda e: e.dma_start(out=cneg[:], in_=lam_d.ap()), [bconst], "d_c0")
        sp(lambda e: e.dma_start(out=nsink[:], in_=sink_d.ap().partition_broadcast(128)), [bconst], "d_c0")
        sp(lambda e: e.dma_start(out=g1c[:], in_=g1c_d.ap()), [bconst], "d_c0")
        sp(lambda e: e.dma_start(out=vcol_s[:], in_=vcol.ap()), [bconst], "d_c0")
        sp(lambda e: e.dma_start(out=gt2[:], in_=g2_d.ap().partition_broadcast(128)), [bgt2], "d_c1")
        sp(lambda e: e.dma_start(out=gt3[:], in_=g3_d.ap().partition_broadcast(128)), [bgt3], "d_c2")
        sp(lambda e: e.dma_start(out=gt4[:], in_=g4_d.ap().partition_broadcast(128)), [bgt4], "d_c3")
        S.add("pool", lambda e: e.dma_start(out=Wa[:], in_=wa_bd.ap().rearrange("j p m -> p j m")), (), [bWa], dma_key="d_wa")
        S.add("pool", lambda e: e.dma_start(out=Wx[:], in_=wx_bd.ap().rearrange("j p m -> p j m")), (), [bWx], dma_key="d_wx")
        win_v = w_in.ap().rearrange("(k p) n -> p k n", p=128)
        stage = ftok
        for k in range(8):
            t = k % 4
            S.add("sp", lambda e, k=k, t=t: e.dma_start(out=stage[:, t, :], in_=win_v[:, k, 0:1024]), (), [bftok[t]], dma_key=f"d_st{t}")
            dve(lambda e, k=k, t=t: e.tensor_scalar(out=Win[:, k, 0:1024], in0=stage[:, t, :], scalar1=g1c[:, k:k + 1], scalar2=None, op0=ALU.mult), [bftok[t], bconst], [bWin[k]])
            S.add("sp", lambda e, k=k, t=t: e.dma_start(out=stage[:, t, 0:768], in_=win_v[:, k, 1024:1792]), (), [bftok[t]], dma_key=f"d_st{t}")
            dve(lambda e, k=k, t=t: e.tensor_scalar(out=Win[:, k, 1024:1792], in0=stage[:, t, 0:768], scalar1=g1c[:, k:k + 1], scalar2=None, op0=ALU.mult), [bftok[t], bconst], [bWin[k]])
        pool(lambda e: e.memset(ones_f[:], 1.0), [], [bident])
        pool(lambda e: e.affine_select(out=ident_f[:], in_=ones_f[:], pattern=[[-1, 128]], compare_op=ALU.is_equal, fill=0.0, base=0, channel_multiplier=1), [bident], [bident])
        pool(lambda e: e.tensor_copy(out=ident[:], in_=ident_f[:]), [bident], [bident])
        mview = maskPT[:].rearrange("p (a b c) -> p a b c", a=2, b=2)
        pool(lambda e: e.affine_select(out=mtmp[:], in_=ones_f[:], pattern=[[-1, 128]], compare_op=ALU.is_ge, fill=0.0, base=-1, channel_multiplier=1), [bident], [bmask])
        for a in range(2):
            pool(lambda e, a=a: e.tensor_copy(out=mview[:, a, 0, :], in_=mtmp[:]), [bmask], [bmask])
        pool(lambda e: e.affine_select(out=mtmp[:], in_=ones_f[:], pattern=[[1, 128]], compare_op=ALU.is_ge, fill=0.0, base=0, channel_multiplier=-1), [bident, bmask], [bmask])
        for a in range(2):
            pool(lambda e, a=a: e.tensor_copy(out=mview[:, a, 1, :], in_=mtmp[:]), [bmask], [bmask])
        pool(lambda e: e.memset(KA[:], 0.0), [], bK)
        pool(lambda e: e.memset(KB[:], 0.0), [], bK)
        pool(lambda e: e.memset(Vaug[:].rearrange("p a b c -> p (a b c)"), 1.0), [], bV)
        pool(lambda e: e.memset(hist[:].rearrange("p a b -> p (a b)"), 0.0), [], bhist)
        pool(lambda e: e.memset(hstate[:], 0.0), [], bhst)
        act(lambda e: e.activation(out=cneg[:], in_=cneg[:], func=AF.Exp, scale=-1.0), [bconst], [bconst])
        act(lambda e: e.activation(out=cneg[:], in_=cneg[:], func=AF.Ln, bias=1.0), [bconst], [bconst])
        dve(lambda e: e.tensor_scalar(out=c2[:], in0=cneg[:], scalar1=-16.0, scalar2=None, op0=ALU.mult), [bconst], [bconst])
        dve(lambda e: e.tensor_scalar(out=cneg[:], in0=cneg[:], scalar1=-8.0, scalar2=None, op0=ALU.mult), [bconst], [bconst])
        dve(lambda e: e.tensor_scalar(out=nba[:], in0=nba[:], scalar1=-1.0, scalar2=None, op0=ALU.mult), [bconst], [bconst])
        dve(lambda e: e.tensor_scalar(out=nbx[:], in0=nbx[:], scalar1=-1.0, scalar2=None, op0=ALU.mult), [bconst], [bconst])
        dve(lambda e: e.tensor_scalar(out=nsink[:], in0=nsink[:], scalar1=-1.0, scalar2=None, op0=ALU.mult), [bconst], [bconst])

    def rstd_from_ms(col):
        act(lambda e: e.activation(out=st_rs[:, col:col + 1], in_=st_ms[:, col:col + 1], func=AF.Ln, bias=EPS), [bst[col]], [bst[col]])
        act(lambda e: e.activation(out=st_rs[:, col:col + 1], in_=st_rs[:, col:col + 1], func=AF.Exp, scale=-0.5), [bst[col]], [bst[col]])

    def norm_transpose(src_ap, bsrc, gtile, bg, dstT, bdst, t, back):
        col = (statB if back else statF).next()
        ui = 2 if back else (t % 2)
        u = bub[ui]
        ubt = ub[ui]
        act(lambda e: e.activation(out=ubt[:], in_=src_ap, func=AF.Square, scale=1.0 / 32.0, accum_out=st_ms[:, col:col + 1]), [bsrc], [u, bst[col]])
        rstd_from_ms(col)
        if gtile is None:
            dve(lambda e: e.tensor_scalar(out=ubt[:], in0=src_ap, scalar1=st_rs[:, col:col + 1], scalar2=None, op0=ALU.mult), [bsrc, bst[col]], [u])
        else:
            dve(lambda e: e.scalar_tensor_tensor(out=ubt[:], in0=src_ap, scalar=st_rs[:, col:col + 1], in1=gtile[:], op0=ALU.mult, op1=ALU.mult), [bsrc, bst[col], bg], [u])
        bk = (bankB if back else bankF).next()
        for k in range(8):
            pe(lambda e, k=k, bk=bk: e.transpose(out=psb[:, bk, k * 128:(k + 1) * 128], in_=ubt[:, k * 128:(k + 1) * 128], identity=ident[:]), [u, bident], [bps[bk]], c=128)
        dve(lambda e, bk=bk: e.tensor_copy(out=dstT[:, :, t * 128:(t + 1) * 128], in_=psb[:, bk, :].rearrange("p (k t) -> p k t", k=8)), [bps[bk]], [bdst[t]])

    def proj_chunk(col0, n, ntile, uTb=None, buTb=None):
        uTb = uT if uTb is None else uTb
        buTb = buT if buTb is None else buTb
        bk = bankF.next()
        for k in range(8):
            pe(lambda e, k=k, bk=bk: e.matmul(ps[:, bk, 0:n], lhsT=Win[:, k, col0:col0 + 128], rhs=uTb[:, k, 0:n], start=(k == 0), stop=(k == 7)),
               [bWin[k]] + buTb[:ntile], [bps[bk]], c=n)
        return bk

    f1T_f = f1T.bitcast(F32)

    def carve(i):
        return f1T_f[:, 2 * i:2 * i + 2, :].rearrange("p a b -> p (a b)")

    class TSet:
        pass

    tsets = []
    for p_ in range(2):
        t_ = TSet()
        t_.T = [TT[p_][i][:, :] for i in range(5)]
        t_.bT = [[bTT[p_][i]] for i in range(5)]
        t_.xr, t_.bxr = xrs[p_][:, :], [bxrs[p_]]
        t_.xcb, t_.bxcb = xcbs[p_][:, :], [bxcbs[p_]]
        tsets.append(t_)
    for p_ in range(2):
        t_ = TSet()
        t_.T = [carve(5 * p_ + i) for i in range(5)]
        t_.bT = [[bf1T[2 * (5 * p_ + i)], bf1T[2 * (5 * p_ + i) + 1]] for i in range(5)]
        r0_ = 20 + 3 * p_
        t_.xr = f1T_f[:, r0_:r0_ + 3, :].rearrange("p a b -> p (a b)")[:, 0:G + 3]
        t_.bxr = [bf1T[r0_], bf1T[r0_ + 1], bf1T[r0_ + 2]]
        t_.xcb, t_.bxcb = f1T[:, 26 + p_, :], [bf1T[26 + p_]]
        tsets.append(t_)

    def lru_chunk(j, n, ntile, is_main, masked, ts, uTb=None, buTb=None, vmo=G):
        N = n
        uTb = uT if uTb is None else uTb
        buTb = buT if buTb is None else buTb
        T = ts.T
        bT = ts.bT
        xcb, bxcb = ts.xcb, ts.bxcb
        bk = proj_chunk(768 + 128 * j, n, ntile, uTb, buTb)
        xr_t = ts.xr
        bxr = ts.bxr
        def L(*xs):
            o = []
            for x in xs:
                if isinstance(x, list):
                    o.extend(x)
                else:
                    o.append(x)
            return o
        dve(lambda e: e.tensor_copy(out=xr_t[:, 0:3], in_=hist[:, j, :]), [bhist[j]], L(bxr))
        act(lambda e, bk=bk: e.copy(out=xr_t[:, 3:3 + N], in_=ps[:, bk, 0:N]), [bps[bk]], L(bxr))
        dve(lambda e: e.tensor_copy(out=hist[:, j, :], in_=xr_t[:, N:N + 3]), L(bxr), [bhist[j]])
        yield 1
        xc = T[0]
        dve(lambda e: e.tensor_scalar(out=xc[:, 0:N], in0=xr_t[:, 3:3 + N], scalar1=cw[:, 4 * j + 3:4 * j + 4], scalar2=cb[:, j:j + 1], op0=ALU.mult, op1=ALU.add), L(bxr, bconst), L(bT[0]))
        for tap in range(3):
            dve(lambda e, tap=tap: e.scalar_tensor_tensor(out=xc[:, 0:N], in0=xr_t[:, tap:tap + N], scalar=cw[:, 4 * j + tap:4 * j + tap + 1], in1=xc[:, 0:N], op0=ALU.mult, op1=ALU.add), L(bxr, bconst, bT[0]), L(bT[0]))
        dve(lambda e: e.tensor_copy(out=xcb[:, 0:N], in_=xc[:, 0:N]), L(bT[0]), L(bxcb))
        yield 1
        bkr = bankF.next()
        pe(lambda e, bkr=bkr: e.matmul(ps[:, bkr, 0:N], lhsT=Wa[:, j, :], rhs=xcb[:, 0:N], start=True, stop=True), L(bWa, bxcb), [bps[bkr]], c=N)
        bki = bankF.next()
        pe(lambda e, bki=bki: e.matmul(ps[:, bki, 0:N], lhsT=Wx[:, j, :], rhs=xcb[:, 0:N], start=True, stop=True), L(bWx, bxcb), [bps[bki]], c=N)
        er, ei, a_, a2 = T[1], T[2], T[3], T[4]
        act(lambda e, bkr=bkr: e.activation(out=er[:, 0:N], in_=ps[:, bkr, 0:N], func=AF.Exp, scale=-1.0, bias=nba[:, j:j + 1]), [bps[bkr], bconst], L(bT[1]))
        act(lambda e, bki=bki: e.activation(out=ei[:, 0:N], in_=ps[:, bki, 0:N], func=AF.Exp, scale=-1.0, bias=nbx[:, j:j + 1]), [bps[bki], bconst], L(bT[2]))
        yield 1
        act(lambda e: e.activation(out=er[:, 0:N], in_=er[:, 0:N], func=AF.Ln, bias=1.0), L(bT[1]), L(bT[1]))
        act(lambda e: e.activation(out=ei[:, 0:N], in_=ei[:, 0:N], func=AF.Ln, bias=1.0), L(bT[2]), L(bT[2]))
        yield 1
        act(lambda e: e.activation(out=er[:, 0:N], in_=er[:, 0:N], func=AF.Exp, scale=-1.0), L(bT[1]), L(bT[1]))
        yield 1
        act(lambda e: e.activation(out=a2[:, 0:N], in_=er[:, 0:N], func=AF.Exp, scale=c2[:, j:j + 1]), L(bT[1], bconst), L(bT[4]))
        act(lambda e: e.activation(out=a_[:, 0:N], in_=er[:, 0:N], func=AF.Exp, scale=cneg[:, j:j + 1]), L(bT[1], bconst), L(bT[3]))
        yield 1
        act(lambda e: e.activation(out=a2[:, 0:N], in_=a2[:, 0:N], func=AF.Ln, scale=-1.0, bias=1.0), L(bT[4]), L(bT[4]))
        yield 1
        dve(lambda e: e.scalar_tensor_tensor(out=a2[:, 0:N], in0=a2[:, 0:N], scalar=0.5, in1=ei[:, 0:N], op0=ALU.mult, op1=ALU.subtract), L(bT[4], bT[2]), L(bT[4]))
        yield 1
        act(lambda e: e.activation(out=a2[:, 0:N], in_=a2[:, 0:N], func=AF.Exp), L(bT[4]), L(bT[4]))
        yield 1
        pool(lambda e: e.tensor_tensor(out=xc[:, 0:N], in0=a2[:, 0:N], in1=xc[:, 0:N], op=ALU.mult), L(bT[4], bT[0]), L(bT[0]))
        if masked:
            pool(lambda e: e.tensor_tensor(out=xc[:, 0:N], in0=xc[:, 0:N], in1=tmp2[:, vmo:vmo + N], op=ALU.mult), L(bT[0], bvmh[vmo // G]), L(bT[0]))
        yield 1
        h_ = T[2]
        dve(lambda e: e.tensor_tensor_scan(out=h_[:, 0:N], data0=a_[:, 0:N], data1=xc[:, 0:N], initial=hstate[:, j:j + 1], op0=ALU.mult, op1=ALU.add), L(bT[3], bT[0], bhst[j]), L(bT[2]))
        dve(lambda e: e.tensor_copy(out=hstate[:, j:j + 1], in_=h_[:, N - 1:N]), L(bT[2]), [bhst[j]])
        yield 1
        if is_main:
            bky = proj_chunk(1280 + 128 * j, n, ntile, uTb, buTb)
            yr, y2 = T[4], T[1]
            act(lambda e, bky=bky: e.copy(out=yr[:, 0:N], in_=ps[:, bky, 0:N]), [bps[bky]], L(bT[4]))
            act(lambda e, bky=bky: e.activation(out=y2[:, 0:N], in_=ps[:, bky, 0:N], func=AF.Square), [bps[bky]], L(bT[1]))
            yield 1
            dve(lambda e: e.tensor_scalar(out=y2[:, 0:N], in0=y2[:, 0:N], scalar1=0.044715, scalar2=1.0, op0=ALU.mult, op1=ALU.add), L(bT[1]), L(bT[1]))
            dve(lambda e: e.tensor_tensor(out=y2[:, 0:N], in0=y2[:, 0:N], in1=yr[:, 0:N], op=ALU.mult), L(bT[1], bT[4]), L(bT[1]))
            yield 1
            act(lambda e: e.activation(out=y2[:, 0:N], in_=y2[:, 0:N], func=AF.Exp, scale=-2.0 * GELU_C), L(bT[1]), L(bT[1]))
            yield 1
            act(lambda e: e.activation(out=y2[:, 0:N], in_=y2[:, 0:N], func=AF.Ln, bias=1.0), L(bT[1]), L(bT[1]))
            yield 1
            act(lambda e: e.activation(out=y2[:, 0:N], in_=y2[:, 0:N], func=AF.Exp, scale=-1.0), L(bT[1]), L(bT[1]))
            yield 1
            pool(lambda e: e.tensor_tensor(out=yr[:, 0:N], in0=yr[:, 0:N], in1=y2[:, 0:N], op=ALU.mult), L(bT[4], bT[1]), L(bT[4]))
            yield 1
            dve(lambda e: e.tensor_tensor(out=recT[:, j, 0:N], in0=yr[:, 0:N], in1=h_[:, 0:N], op=ALU.mult), L(bT[4], bT[2]), [brecT[j]])
            yield 1

    def kv_evac(ntile, blk0, halo, uTb=None, buTb=None):
        uTb = uT if uTb is None else uTb
        buTb = buT if buTb is None else buTb
        n = ntile * 128
        bk = proj_chunk(512, n, ntile, uTb, buTb)
        kbufs = bK[blk0:blk0 + ntile]
        act(lambda e, bk=bk: e.copy(out=KA[0:64, blk0 * 128:blk0 * 128 + n], in_=ps[0:64, bk, 0:n]), [bps[bk]], kbufs)
        act(lambda e, bk=bk: e.copy(out=KB[64:128, blk0 * 128:blk0 * 128 + n], in_=ps[64:128, bk, 0:n]), [bps[bk]], kbufs)
        bv = bankF.next()
        for t in range(ntile):
            for k in range(8):
                pe(lambda e, k=k, t=t, bv=bv: e.matmul(ps[:, bv, t * 128:(t + 1) * 128], lhsT=uTb[:, k, t * 128:(t + 1) * 128], rhs=Win[:, k, 640:768], start=(k == 0), stop=(k == 7)),
                   [bWin[k], buTb[t]], [bps[bv]], c=128)
        for t in range(ntile):
            src = ps[:, bv, t * 128:(t + 1) * 128].rearrange("p (a b) -> p a b", a=2)
            if halo:
                dve(lambda e, t=t, src=src: e.tensor_scalar(out=Vaug[:, blk0 + t, :, 0:64], in0=src, scalar1=vcol_s[:, 0:1], scalar2=None, op0=ALU.mult), [bps[bv], bconst], [bV[blk0 + t]])
                dve(lambda e, t=t: e.tensor_scalar(out=Vaug[:, blk0 + t, :, 64:65], in0=Vaug[:, blk0 + t, :, 64:65], scalar1=vcol_s[:, 0:1], scalar2=None, op0=ALU.mult), [bV[blk0 + t], bconst], [bV[blk0 + t]])
            else:
                dve(lambda e, t=t, src=src: e.tensor_copy(out=Vaug[:, blk0 + t, :, 0:64], in_=src), [bps[bv]], [bV[blk0 + t]])

    def rr(*gens):
        gens = [g for g in gens if g is not None]
        while gens:
            for g in list(gens):
                try:
                    next(g)
                    yield 1
                except StopIteration:
                    gens.remove(g)

    def rr_w(ga, na, gb):
        da = False
        db = gb is None
        while not (da and db):
            for _ in range(na):
                if da:
                    break
                try:
                    next(ga)
                    yield 1
                except StopIteration:
                    da = True
            if not db:
                try:
                    next(gb)
                    yield 1
                except StopIteration:
                    db = True

    def skewed(factories, D):
        active = []
        nxt = 0
        r = 0
        while nxt < len(factories) or active:
            if r % D == 0 and nxt < len(factories):
                active.append(factories[nxt]())
                nxt += 1
            for g_ in list(active):
                try:
                    next(g_)
                except StopIteration:
                    active.remove(g_)
            r += 1
            yield 1

    def seq(*gens):
        for g in gens:
            if g is not None:
                yield from g

    uTbufs = [(uT, buT), (u2T, bu2T)]

    def prefix_m1(pg):
        ntile = 4 if pg < 8 else 1
        tok0 = pg * G
        uTb, buTb = uTbufs[pg % 2]
        for t in range(ntile):
            r0 = tok0 + t * 128
            S.add("sp", lambda e, t=t, r0=r0: e.dma_start(out=xt[t % 2][:], in_=xp.ap()[r0:r0 + 128, :]), (), [bxt[t % 2]], dma_key=f"d_x{t % 2}")
            norm_transpose(xt[t % 2][:], bxt[t % 2], None, None, uTb, buTb, t, False)
            yield 1

    def run_prefix(pgs):
        D = 3
        ntl = lambda pg: 4 if pg < 8 else 1

        def vm_load(gi):
            pg = pgs[gi]
            n = ntl(pg) * 128
            off = (gi % 2) * G
            S.add("sp", lambda e: e.dma_start(out=tmp2[:, off:off + n], in_=vrow.ap()[0:1, pg * G:pg * G + n].partition_broadcast(128)), (), [bvmh[gi % 2]], dma_key=f"d_vm{gi % 2}")

        facts = []
        for gi, pg in enumerate(pgs):
            for j in range(4):
                i = 4 * gi + j
                uTb, buTb = uTbufs[pg % 2]
                facts.append(lambda j=j, pg=pg, i=i, uTb=uTb, buTb=buTb, gi=gi: lru_chunk(j, ntl(pg) * 128, ntl(pg), False, True, tsets[i % 4], uTb, buTb, (gi % 2) * G))
        m1g = {gi: prefix_m1(pg) for gi, pg in enumerate(pgs)}
        vm_load(0)
        for _ in m1g[0]:
            pass
        active = []
        nxt = 0
        r = 0
        while nxt < len(facts) or active:
            gi_n = (r + 4) // (4 * D)
            if 1 <= gi_n < len(pgs):
                t_ = (r + 4) - gi_n * 4 * D
                if t_ == 0:
                    vm_load(gi_n)
                if 0 <= t_ < 4:
                    try:
                        next(m1g[gi_n])
                    except StopIteration:
                        pass
            if r % D == 0 and nxt < len(facts):
                active.append(facts[nxt]())
                nxt += 1
            for g_ in list(active):
                try:
                    next(g_)
                except StopIteration:
                    active.remove(g_)
            r += 1
        if pgs[-1] == 8:
            uTb, buTb = uTbufs[0]
            kv_evac(1, 0, True, uTb, buTb)

    def front_a(g):
        for t in range(4):
            r0 = g * G + t * 128
            S.add("sp", lambda e, t=t, r0=r0: e.dma_start(out=xt[t % 2][:], in_=xm.ap()[r0:r0 + 128, :]), (), [bxt[t % 2]], dma_key=f"d_x{t % 2}")
            norm_transpose(xt[t % 2][:], bxt[t % 2], None, None, uT, buT, t, False)
            yield 1
        for c in range(4):
            bk = proj_chunk(128 * c, G, 4)
            act(lambda e, bk=bk, c=c: e.copy(out=qT[:, c, :], in_=ps[:, bk, :]), [bps[bk]], [bqT[c]])
            yield 1
        if g > 0:
            dve(lambda e: e.tensor_copy(out=KA[0:64, 0:128], in_=KA[0:64, 512:640]), [bK[4]], [bK[0]])
            dve(lambda e: e.tensor_copy(out=KB[64:128, 0:128], in_=KB[64:128, 512:640]), [bK[4]], [bK[0]])
            dve(lambda e: e.tensor_copy(out=Vaug[:, 0, :, :], in_=Vaug[:, 4, :, :]), [bV[4]], [bV[0]])
        kv_evac(4, 1, False)
        yield 1

    def front_b(g):
        lru = skewed([lambda j=j: lru_chunk(j, G, 4, True, False, tsets[j % 2]) for j in range(4)], 9)
        import os as _os
        if _os.environ.get("LRU_SEQ") == "1":
            for j in range(4):
                yield from lru_chunk(j, G, 4, True, False, tsets[j % 2])
            yield from attention(g)
        else:
            yield from rr(lru, attention(g))

    def attention(g):
        for n in range(4):
            b_o = [3, 4]
            def s_part(c):
                bs = bankF.next()
                pt = PT[c % 2]
                bpt = bPT[c % 2]
                qs = qT[:, c, n * 128:(n + 1) * 128]
                for a, Kt in enumerate((KA, KB)):
                    for pc in range(2):
                        blk = n + pc
                        col = (a * 2 + pc) * 128
                        pe(lambda e, Kt=Kt, blk=blk, col=col, bs=bs, qs=qs: e.matmul(ps[:, bs, col:col + 128], lhsT=Kt[:, blk * 128:(blk + 1) * 128], rhs=qs, start=True, stop=True),
                           [bK[blk], bqT[c]], [bps[bs]], c=128)
                for a in range(2):
                    h = c + 4 * a
                    act(lambda e, a=a, h=h, bs=bs, pt=pt: e.activation(out=pt[:, a * 256:(a + 1) * 256], in_=ps[:, bs, a * 256:(a + 1) * 256], func=AF.Exp, scale=0.125, bias=nsink[:, h:h + 1]),
                        [bps[bs], bconst], [bpt])
                dve(lambda e, pt=pt: e.tensor_tensor(out=pt[:], in0=pt[:], in1=maskPT[:], op=ALU.mult), [bpt, bmask], [bpt])

            def pv_part(c):
                pt = PT[c % 2]
                bpt = bPT[c % 2]
                for a in range(2):
                    h = c + 4 * a
                    bo = b_o[h // 4]
                    o0 = (h % 4) * 65
                    for pc in range(2):
                        blk = n + pc
                        col = (a * 2 + pc) * 128
                        pe(lambda e, a=a, blk=blk, col=col, bo=bo, o0=o0, pt=pt, pc=pc: e.matmul(ps[:, bo, o0:o0 + 65], lhsT=pt[:, col:col + 128], rhs=Vaug[:, blk, a, :], start=(pc == 0), stop=(pc == 1)),
                           [bpt, bV[blk]], [bps[bo]], c=65)

            s_part(0)
            yield 1
            for c in range(4):
                if c + 1 < 4:
                    s_part(c + 1)
                    yield 1
                pv_part(c)
                yield 1
            at = atok[n % 2]
            bat = batok[n % 2]
            for hh in range(2):
                bo = b_o[hh]
                ov = ps[:, bo, 0:260].rearrange("p (h d) -> p h d", h=4)
                dve(lambda e, ov=ov, hh=hh: e.tensor_scalar(out=den[:, hh * 4:hh * 4 + 4], in0=ov[:, :, 64], scalar1=1.0, scalar2=None, op0=ALU.add), [bps[bo]], [bden])
                dve(lambda e, hh=hh: e.reciprocal(out=den[:, hh * 4:hh * 4 + 4], in_=den[:, hh * 4:hh * 4 + 4]), [bden], [bden])
                for hq in range(4):
                    h = hh * 4 + hq
                    dve(lambda e, ov=ov, hq=hq, h=h, at=at: e.tensor_scalar(out=at[:, h * 64:(h + 1) * 64], in0=ov[:, hq, 0:64], scalar1=den[:, h:h + 1], scalar2=None, op0=ALU.mult), [bps[bo], bden], [bat])
            bt = bankF.next()
            for k in range(4):
                pe(lambda e, k=k, bt=bt, at=at: e.transpose(out=psb[:, bt, k * 128:(k + 1) * 128], in_=at[:, k * 128:(k + 1) * 128], identity=ident[:]), [bat, bident], [bps[bt]], c=128)
            act(lambda e, bt=bt, n=n: e.copy(out=attnT[:, :, n * 128:(n + 1) * 128], in_=psb[:, bt, 0:512].rearrange("p (k t) -> p k t", k=4)), [bps[bt]], [battnT[n]])
            yield 1

    def wload(slot, src_ap, view):
        return S.add("pool", lambda e: e.dma_start(out=view, in_=src_ap), (), [bwst[slot]], dma_key=f"d_w{slot}")

    w1v = w_ff1.ap().rearrange("(k p) n -> p k n", p=128)
    w2v = w_ff2.ap().rearrange("(m p) n -> p m n", p=128)
    wov = w_out.ap().rearrange("(k p) n -> p k n", p=128)

    def slot_w1(s):
        return wst[s][:].rearrange("p (k n) -> p k n", k=8)

    def slot_w2(s):
        return wst[s][:].rearrange("p (m n) -> p m n", m=32)

    def back_m5(g):
        if g == 0:
            for hf in range(2):
                wload(hf, wov[:, :, hf * 512:(hf + 1) * 512], slot_w1(hf))
        for t in range(4):
            r0 = g * G + t * 128
            S.add("sp", lambda e, t=t, r0=r0: e.dma_start(out=h1[:, t, :], in_=xm.ap()[r0:r0 + 128, :]), (), [bh1[t]], dma_key=f"d_h{t}")
            for hf in range(2):
                bk = bankB.next()
                for k in range(8):
                    src = attnT[:, k, t * 128:(t + 1) * 128] if k < 4 else recT[:, k - 4, t * 128:(t + 1) * 128]
                    bsrc = battnT[t] if k < 4 else brecT[k - 4]
                    pe(lambda e, k=k, bk=bk, src=src, hf=hf: e.matmul(ps[:, bk, :], lhsT=src, rhs=slot_w1(hf)[:, k, :], start=(k == 0), stop=(k == 7)),
                       [bsrc, bwst[hf]], [bps[bk]])
                act(lambda e, bk=bk, t=t, hf=hf: e.copy(out=ftok[:, t, hf * 512:(hf + 1) * 512], in_=ps[:, bk, :]), [bps[bk]], [bftok[t]])
            yield 1100
        wload(0, w1v[:, :, 0:512], slot_w1(0))
        wload(1, w1v[:, :, 512:1024], slot_w1(1))
        state["allow_fb"] = True
        for t in range(4):
            col = statB.next()
            act(lambda e, t=t, col=col: e.activation(out=ub[2][:], in_=ftok[:, t, :], func=AF.Square, scale=1.0 / 32.0, accum_out=st_ms[:, col:col + 1]), [bftok[t]], [bub[2], bst[col]])
            rstd_from_ms(col)
            dve(lambda e, t=t, col=col: e.scalar_tensor_tensor(out=ftok[:, t, :], in0=ftok[:, t, :], scalar=st_rs[:, col:col + 1], in1=gt2[:], op0=ALU.mult, op1=ALU.mult), [bftok[t], bst[col], bgt2], [bftok[t]])
            dve(lambda e, t=t: e.tensor_tensor(out=h1[:, t, :], in0=h1[:, t, :], in1=ftok[:, t, :], op=ALU.add), [bh1[t], bftok[t]], [bh1[t]])
            norm_transpose(h1[:, t, :], bh1[t], gt3, bgt3, u2T, bu2T, t, True)
            yield 14000

    def back_ffn(g):
        for fg in range(8):
            s = fg % 2
            for m in range(4):
                bk = bankB.next()
                mm = fg * 4 + m
                for k in range(8):
                    pe(lambda e, k=k, bk=bk, s=s, m=m: e.matmul(ps[:, bk, :], lhsT=slot_w1(s)[:, k, m * 128:(m + 1) * 128], rhs=u2T[:, k, :], start=(k == 0), stop=(k == 7)),
                       [bwst[s]] + bu2T, [bps[bk]])
                act(lambda e, bk=bk, mm=mm: e.activation(out=f1T[:, mm, :], in_=ps[:, bk, :], func=AF.Relu), [bps[bk]], [bf1T[mm]])
                dve(lambda e, mm=mm: e.tensor_tensor(out=f1T[:, mm, :], in0=f1T[:, mm, :], in1=f1T[:, mm, :], op=ALU.mult), [bf1T[mm]], [bf1T[mm]])
                yield 710
            nxt = fg + 2
            if nxt < 8:
                wload(s, w1v[:, :, nxt * 512:(nxt + 1) * 512], slot_w1(s))
            else:
                oc = nxt - 8
                wload(s, w2v[:, :, oc * 128:(oc + 1) * 128], slot_w2(s))
        for oc in range(8):
            s = oc % 2
            bk = bankB.next()
            for m in range(32):
                pe(lambda e, m=m, bk=bk, s=s: e.matmul(ps[:, bk, :], lhsT=slot_w2(s)[:, m, :], rhs=f1T[:, m, :], start=(m == 0), stop=(m == 31)),
                   [bwst[s], bf1T[m]], [bps[bk]])
                if m % 8 == 7:
                    yield 710
            if oc + 2 < 8:
                wload(s, w2v[:, :, (oc + 2) * 128:(oc + 3) * 128], slot_w2(s))
            act(lambda e, bk=bk: e.copy(out=fTc, in_=ps[:, bk, :]), [bps[bk]], [bfTc])
            bt = bankB.next()
            for t in range(4):
                pe(lambda e, t=t, bt=bt: e.transpose(out=ps[:, bt, t * 128:(t + 1) * 128], in_=tmp2[:, t * 128:(t + 1) * 128], identity=ident_f[:]), [bfTc, bident], [bps[bt]], c=128)
            dve(lambda e, bt=bt, oc=oc: e.tensor_copy(out=ftok[:, :, oc * 128:(oc + 1) * 128], in_=ps[:, bt, :].rearrange("p (t f) -> p t f", t=4)), [bps[bt]], bftok)
            yield 710
        if g + 1 < n_main_groups:
            for hf in range(2):
                wload(hf, wov[:, :, hf * 512:(hf + 1) * 512], slot_w1(hf))

    def back_final(g):
        for t in range(4):
            col = statB.next()
            act(lambda e, t=t, col=col: e.activation(out=ub[2][:], in_=ftok[:, t, :], func=AF.Square, scale=1.0 / 32.0, accum_out=st_ms[:, col:col + 1]), [bftok[t]], [bub[2], bst[col]])
            rstd_from_ms(col)
            dve(lambda e, t=t, col=col: e.scalar_tensor_tensor(out=ftok[:, t, :], in0=ftok[:, t, :], scalar=st_rs[:, col:col + 1], in1=gt4[:], op0=ALU.mult, op1=ALU.mult), [bftok[t], bst[col], bgt4], [bftok[t]])
            dve(lambda e, t=t: e.tensor_tensor(out=h1[:, t, :], in0=h1[:, t, :], in1=ftok[:, t, :], op=ALU.add), [bh1[t], bftok[t]], [bh1[t]])
            r0 = g * G + t * 128
            out_ops.append(S.add("sp", lambda e, t=t, r0=r0: e.dma_start(out=out_d.ap()[r0:r0 + 128, :], in_=h1[:, t, :]), [bh1[t]], (), dma_key=f"d_o{t}"))
            yield 0

    def run_alone(gen, who="A"):
        state["cur"] = who
        for _ in gen:
            pass

    def merge(genA, totA, genB, totB):
        a0, b0 = cost["A"], cost["B"]
        doneA = genA is None
        doneB = genB is None
        while not (doneA and doneB):
            pa = (cost["A"] - a0) / totA
            pb = (cost["B"] - b0) / totB
            if (not doneA) and (doneB or pa <= pb):
                state["cur"] = "A"
                try:
                    next(genA)
                except StopIteration:
                    doneA = True
            else:
                state["cur"] = "B"
                try:
                    next(genB)
                except StopIteration:
                    doneB = True

    def zip4(ga, gb):
        if ga is not None:
            for _ in range(4):
                yield next(ga)
                yield next(gb)
            for v in ga:
                yield v
        yield from gb

    def merge2(A_gens, genB):
        ai = 0
        budget = 0.0
        state["allow_fb"] = False
        while True:
            state["cur"] = "B"
            try:
                budget += next(genB)
            except StopIteration:
                break
            state["cur"] = "A"
            while budget > 0 and ai < len(A_gens):
                if ai >= 1 and not state["allow_fb"]:
                    break
                c0 = cost["A"]
                try:
                    next(A_gens[ai])
                except StopIteration:
                    ai += 1
                    continue
                budget -= (cost["A"] - c0) + 50
        state["cur"] = "A"
        while ai < len(A_gens):
            for _ in A_gens[ai]:
                pass
            ai += 1

    setup()
    pgs = list(range(9 - n_pre_groups, 9))
    state["cur"] = "A"
    run_prefix(pgs)
    if pipelined:
        run_alone(front_a(0))
        run_alone(front_b(0))
        for g in range(n_main_groups):
            nxt = g + 1 < n_main_groups
            Bs = seq(zip4(back_final(g - 1) if g > 0 else None, back_m5(g)), back_ffn(g))
            merge2([front_a(g + 1), front_b(g + 1)] if nxt else [], Bs)
        run_alone(back_final(n_main_groups - 1), "B")
    else:
        for g in range(n_main_groups):
            run_alone(front_a(g))
            run_alone(front_b(g))
            run_alone(back_m5(g), "B")
            run_alone(back_ffn(g), "B")
            run_alone(back_final(g), "B")
    S.add("sp", None, extra_deps=out_ops)
    print("sbuf bytes remaining", nc.sbuf_bytes_remaining)
    S.emit(nc)
    es.close()
    return nc


_NC_CACHE = {}


def _prep_inputs(x, meta_tokens, g_pre_mix, w_in, conv_w, conv_b, w_a, b_a, w_x, b_x,
                 lru_lambda, attn_sinks, w_out, g_post_mix, g_pre_ffn, w_ff1, w_ff2, g_post_ffn):
    f = np.float32
    x = np.asarray(x, f)
    meta = np.asarray(meta_tokens, f)
    w_in0 = np.asarray(w_in, f)[0]
    order = []
    for c in range(4):
        order += list(range(c * 64, c * 64 + 64)) + list(range((4 + c) * 64, (4 + c) * 64 + 64))
    perm = np.array(order + list(range(512, 1792)))
    w_in_p = np.ascontiguousarray(w_in0[:, perm])

    def bd(w):
        w = np.asarray(w, f)[0]
        o = np.zeros((4, 128, 128), f)
        for j in range(4):
            o[j, 0:64, 0:64] = w[2 * j]
            o[j, 64:128, 64:128] = w[2 * j + 1]
        return o

    def chan(v):
        return np.ascontiguousarray(np.asarray(v, f).reshape(4, 128).T)

    cw = np.asarray(conv_w, f)[0]
    cw_l = np.ascontiguousarray(cw.reshape(4, 4, 128).transpose(2, 1, 0).reshape(128, 16))
    shared = {
        "w_in": w_in_p,
        "w_out": np.ascontiguousarray(np.asarray(w_out, f)[0]),
        "w_ff1": np.ascontiguousarray(np.asarray(w_ff1, f)[0]),
        "w_ff2": np.ascontiguousarray(np.asarray(w_ff2, f)[0]),
        "wa_bd": bd(w_a),
        "wx_bd": bd(w_x),
        "cw": cw_l,
        "cb": chan(np.asarray(conv_b, f)[0]),
        "ba": chan(np.asarray(b_a, f)[0]),
        "bx": chan(np.asarray(b_x, f)[0]),
        "lam": chan(np.asarray(lru_lambda, f)[0]),
        "sinks": np.ascontiguousarray(np.asarray(attn_sinks, f)[0].reshape(1, 8)),
        "g1c": np.ascontiguousarray(np.asarray(g_pre_mix, f)[0].reshape(8, 128).T),
        "g2": np.ascontiguousarray(np.asarray(g_post_mix, f)[0].reshape(1, D)),
        "g3": np.ascontiguousarray(np.asarray(g_pre_ffn, f)[0].reshape(1, D)),
        "g4": np.ascontiguousarray(np.asarray(g_post_ffn, f)[0].reshape(1, D)),
    }
    in_maps = []
    for c in range(8):
        b, hf = c // 2, c % 2
        xp = np.zeros((NPRE, D), f)
        vrow = np.zeros((1, NPRE), f)
        vcol = np.zeros((128, 1), f)
        if hf == 0:
            xp[NPRE - 16:] = meta
            vrow[0, NPRE - 16:] = 1.0
            vcol[112:, 0] = 1.0
        else:
            xp[112:128] = meta
            xp[128:] = x[b, 0:4096]
            vrow[0, 112:] = 1.0
            vcol[:, 0] = 1.0
        m = dict(shared)
        m["xp"] = xp
        m["xm"] = np.ascontiguousarray(x[b, hf * 4096:(hf + 1) * 4096])
        m["vrow"] = vrow
        m["vcol"] = vcol
        in_maps.append(m)
    return in_maps


def kernel(**inputs):
    in_maps = _prep_inputs(**inputs)
    if "nc" not in _NC_CACHE:
        _NC_CACHE["nc"] = build_program()
    nc = _NC_CACHE["nc"]
    res = run_bass_kernel_spmd(nc, in_maps, core_ids=list(range(8)))
    out = np.empty((4, 8192, D), np.float32)
    for c in range(8):
        b, hf = c // 2, c % 2
        out[b, hf * 4096:(hf + 1) * 4096] = res.results[c]["out"]
    return out
```

```python
from contextlib import ExitStack

import numpy as np
import concourse.bass as bass
import concourse.mybir as mybir
from concourse.bass_utils import run_bass_kernel_spmd

F32 = mybir.dt.float32
BF16 = mybir.dt.bfloat16
AF = mybir.ActivationFunctionType
ALU = mybir.AluOpType

D = 1024
NT_MAIN = 32
NPRE = 4224
G = 512
EPS = 1e-6
GELU_C = 0.7978845608028654


class Buf:
    __slots__ = ("name", "last_write", "readers")

    def __init__(self, name):
        self.name = name
        self.last_write = None
        self.readers = {}


class Op:
    __slots__ = ("eng", "fn", "deps", "signals", "key", "tick", "is_dma", "idx")

    def __init__(self, eng, fn, key, is_dma):
        self.eng = eng
        self.fn = fn
        self.key = key
        self.is_dma = is_dma
        self.deps = []
        self.signals = False
        self.tick = 0


ENGS = ("sp", "act", "dve", "pool", "pe")


class Sched:
    def __init__(self):
        self.queues = {e: [] for e in ENGS}
        self.nops = 0

    def add(self, eng, fn, reads=(), writes=(), dma_key=None, extra_deps=()):
        is_dma = dma_key is not None
        key = dma_key if is_dma else eng
        op = Op(eng, fn, key, is_dma)
        op.idx = self.nops
        self.nops += 1
        deps = {}
        for b in reads:
            w = b.last_write
            if w is not None:
                deps[id(w)] = w
        for b in writes:
            w = b.last_write
            if w is not None:
                deps[id(w)] = w
            for r in b.readers.values():
                deps[id(r)] = r
        for d in extra_deps:
            deps[id(d)] = d
        out = []
        for d in deps.values():
            if d is op:
                continue
            if d.key == "pe" and key == "pe":
                continue
            out.append(d)
            d.signals = True
        op.deps = out
        for b in reads:
            b.readers[key] = op
        for b in writes:
            b.last_write = op
            b.readers = {}
        self.queues[eng].append(op)
        return op

    def emit(self, nc):
        counts = {}
        keys = []
        allops = []
        for e in ENGS:
            allops.extend(self.queues[e])
        allops.sort(key=lambda o: o.idx)
        for op in allops:
            if op.key not in counts:
                counts[op.key] = 0
                keys.append(op.key)
            if op.signals:
                counts[op.key] += 16 if op.is_dma else 1
            op.tick = counts[op.key]
        with ExitStack() as es:
            sems = {k: es.enter_context(nc.semaphore("s_" + k)) for k in keys}
            with nc.Block() as block:
                def run(engname, eng):
                    waited = {}
                    for op in self.queues[engname]:
                        need = {}
                        for d in op.deps:
                            if d.tick > need.get(d.key, 0):
                                need[d.key] = d.tick
                        for k, v in need.items():
                            if waited.get(k, 0) < v:
                                eng.wait_ge(sems[k], v)
                                waited[k] = v
                        if op.fn is not None:
                            ins = op.fn(eng)
                            if op.signals:
                                ins.then_inc(sems[op.key], 16 if op.is_dma else 1)

                @block.sync
                def _(eng):
                    run("sp", eng)

                @block.scalar
                def _(eng):
                    run("act", eng)

                @block.vector
                def _(eng):
                    run("dve", eng)

                @block.gpsimd
                def _(eng):
                    run("pool", eng)

                @block.tensor
                def _(eng):
                    run("pe", eng)


def build_program(n_main_groups=8, n_pre_groups=9, pipelined=True):
    nc = bass.Bass("TRN2", target_bir_lowering=False)
    S = Sched()

    def din(name, shape):
        return nc.dram_tensor(name, list(shape), F32, kind="ExternalInput")

    xp = din("xp", [NPRE, D])
    xm = din("xm", [NT_MAIN * 128, D])
    vrow = din("vrow", [1, NPRE])
    vcol = din("vcol", [128, 1])
    w_in = din("w_in", [D, 1792])
    w_out = din("w_out", [D, D])
    w_ff1 = din("w_ff1", [D, 4096])
    w_ff2 = din("w_ff2", [4096, D])
    wa_bd = din("wa_bd", [4, 128, 128])
    wx_bd = din("wx_bd", [4, 128, 128])
    cw_d = din("cw", [128, 16])
    cb_d = din("cb", [128, 4])
    ba_d = din("ba", [128, 4])
    bx_d = din("bx", [128, 4])
    lam_d = din("lam", [128, 4])
    sink_d = din("sinks", [1, 8])
    g1c_d = din("g1c", [128, 8])
    g2_d = din("g2", [1, D])
    g3_d = din("g3", [1, D])
    g4_d = din("g4", [1, D])
    out_d = nc.dram_tensor("out", [NT_MAIN * 128, D], F32, kind="ExternalOutput")

    es = ExitStack()

    def sb(name, shape, dt=F32):
        return es.enter_context(nc.sbuf_tensor("sb_" + name, list(shape), dt))

    def B(name):
        return Buf(name)

    Win = sb("Win", [128, 8, 1792], BF16)
    Wa = sb("Wa", [128, 4, 128], BF16)
    Wx = sb("Wx", [128, 4, 128], BF16)
    gt2 = sb("gt2", [128, D])
    gt3 = sb("gt3", [128, D])
    gt4 = sb("gt4", [128, D])
    wst = [sb(f"wst{i}", [128, 4096], BF16) for i in range(2)]
    xt = [sb(f"xt{i}", [128, D]) for i in range(2)]
    h1 = sb("h1", [128, 4, D])
    ub = [sb(f"ub{i}", [128, D], BF16) for i in range(3)]
    uT = sb("uT", [128, 8, G], BF16)
    u2T = sb("u2T", [128, 8, G], BF16)
    qT = sb("qT", [128, 4, G], BF16)
    KA = sb("KA", [128, 640], BF16)
    KB = sb("KB", [128, 640], BF16)
    Vaug = sb("Vaug", [128, 5, 2, 65], BF16)
    PT = [sb(f"PT{i}", [128, 512], BF16) for i in range(2)]
    maskPT = sb("maskPT", [128, 512], BF16)
    atok = [sb(f"atok{i}", [128, 512], BF16) for i in range(2)]
    attnT = sb("attnT", [128, 4, G], BF16)
    recT = sb("recT", [128, 4, G], BF16)
    xrs = [sb(f"xrs{i}", [128, G + 3]) for i in range(2)]
    TT = [[sb(f"T{p}_{i}", [128, G]) for i in range(5)] for p in range(2)]
    xcbs = [sb(f"xcb{p}", [128, G], BF16) for p in range(2)]
    hist = sb("hist", [128, 4, 3])
    hstate = sb("hstate", [128, 4])
    ftok = sb("ftok", [128, 4, D])
    tmp2 = sb("tmp2", [128, D])
    f1T = sb("f1T", [128, 32, G], BF16)
    fTc = tmp2[:, 0:G]
    vm = tmp2[:, G:2 * G]
    cw = sb("cw", [128, 16])
    cb = sb("cb", [128, 4])
    nba = sb("nba", [128, 4])
    nbx = sb("nbx", [128, 4])
    cneg = sb("cneg", [128, 4])
    c2 = sb("c2", [128, 4])
    nsink = sb("nsink", [128, 8])
    g1c = sb("g1c", [128, 8])
    vcol_s = sb("vcol_s", [128, 1])
    ident_f = sb("ident_f", [128, 128])
    ident = sb("ident", [128, 128], BF16)
    ones_f = sb("ones_f", [128, 128])
    mtmp = sb("mtmp", [128, 128])
    st_ms = sb("st_ms", [128, 8])
    st_rs = sb("st_rs", [128, 8])
    den = sb("den", [128, 8])
    ps = es.enter_context(nc.psum_tensor("ps", [128, 8, 512], F32))
    psb = ps.bitcast(BF16)

    bWin = [B(f"Win{k}") for k in range(8)]
    bWa, bWx = B("Wa"), B("Wx")
    bgt2, bgt3, bgt4 = B("gt2"), B("gt3"), B("gt4")
    bwst = [B("wst0"), B("wst1")]
    bxt = [B("xt0"), B("xt1")]
    bh1 = [B(f"h1_{t}") for t in range(4)]
    bub = [B(f"ub{i}") for i in range(3)]
    buT = [B(f"uT{t}") for t in range(4)]
    bu2T = [B(f"u2T{t}") for t in range(4)]
    bqT = [B(f"qT{c}") for c in range(4)]
    bK = [B(f"K{b}") for b in range(5)]
    bV = [B(f"V{b}") for b in range(5)]
    bPT = [B("PT0"), B("PT1")]
    bmask = B("maskPT")
    batok = [B("atok0"), B("atok1")]
    battnT = [B(f"attnT{t}") for t in range(4)]
    brecT = [B(f"recT{j}") for j in range(4)]
    bxrs = [B("xrs0"), B("xrs1")]
    bTT = [[B(f"T{p}_{i}") for i in range(5)] for p in range(2)]
    bxcbs = [B("xcb0"), B("xcb1")]
    bhist = [B(f"hist{j}") for j in range(4)]
    bhst = [B(f"hst{j}") for j in range(4)]
    bftok = [B(f"ftok{t}") for t in range(4)]
    btmp2 = B("tmp2")
    bf1T = [B(f"f1T{m}") for m in range(32)]
    bfTc = btmp2
    bvm = btmp2
    bvmh = [B("vmh0"), B("vmh1")]
    bconst = B("const")
    bident = B("ident")
    bst = [B(f"st{i}") for i in range(8)]
    bden = B("den")
    bps = [B(f"ps{i}") for i in range(8)]

    class Rot:
        def __init__(self, ids):
            self.ids = list(ids)
            self.i = 0

        def next(self):
            v = self.ids[self.i % len(self.ids)]
            self.i += 1
            return v

    bankF = Rot([0, 1, 2])
    bankB = Rot([5, 6, 7])
    statF = Rot([0, 1, 2, 3])
    statB = Rot([4, 5, 6, 7])
    out_ops = []
    cost = {"A": 0.0, "B": 0.0}
    state = {"cur": "A"}

    def act(fn, reads, writes):
        return S.add("act", fn, reads, writes)

    def dve(fn, reads, writes):
        return S.add("dve", fn, reads, writes)

    def pool(fn, reads, writes):
        return S.add("pool", fn, reads, writes)

    def pe(fn, reads, writes, c=512):
        cost[state["cur"]] += max(c, 64)
        return S.add("pe", fn, reads, writes)

    def setup():
        sp = lambda fn, w, key: S.add("sp", fn, (), w, dma_key=key)
        sp(lambda e: e.dma_start(out=cw[:], in_=cw_d.ap()), [bconst], "d_c0")
        sp(lambda e: e.dma_start(out=cb[:], in_=cb_d.ap()), [bconst], "d_c0")
        sp(lambda e: e.dma_start(out=nba[:], in_=ba_d.ap()), [bconst], "d_c0")
        sp(lambda e: e.dma_start(out=nbx[:], in_=bx_d.ap()), [bconst], "d_c0")
        sp(lambda e: e.dma_start(out=cneg[:], in_=lam_d.ap()), [bconst], "d_c0")
        sp(lambda e: e.dma_start(out=nsink[:], in_=sink_d.ap().partition_broadcast(128)), [bconst], "d_c0")
        sp(lambda e: e.dma_start(out=g1c[:], in_=g1c_d.ap()), [bconst], "d_c0")
        sp(lambda e: e.dma_start(out=vcol_s[:], in_=vcol.ap()), [bconst], "d_c0")
        sp(lambda e: e.dma_start(out=gt2[:], in_=g2_d.ap().partition_broadcast(128)), [bgt2], "d_c1")
        sp(lambda e: e.dma_start(out=gt3[:], in_=g3_d.ap().partition_broadcast(128)), [bgt3], "d_c2")
        sp(lambda e: e.dma_start(out=gt4[:], in_=g4_d.ap().partition_broadcast(128)), [bgt4], "d_c3")
        S.add("pool", lambda e: e.dma_start(out=Wa[:], in_=wa_bd.ap().rearrange("j p m -> p j m")), (), [bWa], dma_key="d_wa")
        S.add("pool", lambda e: e.dma_start(out=Wx[:], in_=wx_bd.ap().rearrange("j p m -> p j m")), (), [bWx], dma_key="d_wx")
        win_v = w_in.ap().rearrange("(k p) n -> p k n", p=128)
        stage = ftok
        for k in range(8):
            t = k % 4
            S.add("sp", lambda e, k=k, t=t: e.dma_start(out=stage[:, t, :], in_=win_v[:, k, 0:1024]), (), [bftok[t]], dma_key=f"d_st{t}")
            dve(lambda e, k=k, t=t: e.tensor_scalar(out=Win[:, k, 0:1024], in0=stage[:, t, :], scalar1=g1c[:, k:k + 1], scalar2=None, op0=ALU.mult), [bftok[t], bconst], [bWin[k]])
            S.add("sp", lambda e, k=k, t=t: e.dma_start(out=stage[:, t, 0:768], in_=win_v[:, k, 1024:1792]), (), [bftok[t]], dma_key=f"d_st{t}")
            dve(lambda e, k=k, t=t: e.tensor_scalar(out=Win[:, k, 1024:1792], in0=stage[:, t, 0:768], scalar1=g1c[:, k:k + 1], scalar2=None, op0=ALU.mult), [bftok[t], bconst], [bWin[k]])
        pool(lambda e: e.memset(ones_f[:], 1.0), [], [bident])
        pool(lambda e: e.affine_select(out=ident_f[:], in_=ones_f[:], pattern=[[-1, 128]], compare_op=ALU.is_equal, fill=0.0, base=0, channel_multiplier=1), [bident], [bident])
        pool(lambda e: e.tensor_copy(out=ident[:], in_=ident_f[:]), [bident], [bident])
        mview = maskPT[:].rearrange("p (a b c) -> p a b c", a=2, b=2)
        pool(lambda e: e.affine_select(out=mtmp[:], in_=ones_f[:], pattern=[[-1, 128]], compare_op=ALU.is_ge, fill=0.0, base=-1, channel_multiplier=1), [bident], [bmask])
        for a in range(2):
            pool(lambda e, a=a: e.tensor_copy(out=mview[:, a, 0, :], in_=mtmp[:]), [bmask], [bmask])
        pool(lambda e: e.affine_select(out=mtmp[:], in_=ones_f[:], pattern=[[1, 128]], compare_op=ALU.is_ge, fill=0.0, base=0, channel_multiplier=-1), [bident, bmask], [bmask])
        for a in range(2):
            pool(lambda e, a=a: e.tensor_copy(out=mview[:, a, 1, :], in_=mtmp[:]), [bmask], [bmask])
        pool(lambda e: e.memset(KA[:], 0.0), [], bK)
        pool(lambda e: e.memset(KB[:], 0.0), [], bK)
        pool(lambda e: e.memset(Vaug[:].rearrange("p a b c -> p (a b c)"), 1.0), [], bV)
        pool(lambda e: e.memset(hist[:].rearrange("p a b -> p (a b)"), 0.0), [], bhist)
        pool(lambda e: e.memset(hstate[:], 0.0), [], bhst)
        act(lambda e: e.activation(out=cneg[:], in_=cneg[:], func=AF.Exp, scale=-1.0), [bconst], [bconst])
        act(lambda e: e.activation(out=cneg[:], in_=cneg[:], func=AF.Ln, bias=1.0), [bconst], [bconst])
        dve(lambda e: e.tensor_scalar(out=c2[:], in0=cneg[:], scalar1=-16.0, scalar2=None, op0=ALU.mult), [bconst], [bconst])
        dve(lambda e: e.tensor_scalar(out=cneg[:], in0=cneg[:], scalar1=-8.0, scalar2=None, op0=ALU.mult), [bconst], [bconst])
        dve(lambda e: e.tensor_scalar(out=nba[:], in0=nba[:], scalar1=-1.0, scalar2=None, op0=ALU.mult), [bconst], [bconst])
        dve(lambda e: e.tensor_scalar(out=nbx[:], in0=nbx[:], scalar1=-1.0, scalar2=None, op0=ALU.mult), [bconst], [bconst])
        dve(lambda e: e.tensor_scalar(out=nsink[:], in0=nsink[:], scalar1=-1.0, scalar2=None, op0=ALU.mult), [bconst], [bconst])

    def rstd_from_ms(col):
        act(lambda e: e.activation(out=st_rs[:, col:col + 1], in_=st_ms[:, col:col + 1], func=AF.Ln, bias=EPS), [bst[col]], [bst[col]])
        act(lambda e: e.activation(out=st_rs[:, col:col + 1], in_=st_rs[:, col:col + 1], func=AF.Exp, scale=-0.5), [bst[col]], [bst[col]])

    def norm_transpose(src_ap, bsrc, gtile, bg, dstT, bdst, t, back):
        col = (statB if back else statF).next()
        ui = 2 if back else (t % 2)
        u = bub[ui]
        ubt = ub[ui]
        act(lambda e: e.activation(out=ubt[:], in_=src_ap, func=AF.Square, scale=1.0 / 32.0, accum_out=st_ms[:, col:col + 1]), [bsrc], [u, bst[col]])
        rstd_from_ms(col)
        if gtile is None:
            dve(lambda e: e.tensor_scalar(out=ubt[:], in0=src_ap, scalar1=st_rs[:, col:col + 1], scalar2=None, op0=ALU.mult), [bsrc, bst[col]], [u])
        else:
            dve(lambda e: e.scalar_tensor_tensor(out=ubt[:], in0=src_ap, scalar=st_rs[:, col:col + 1], in1=gtile[:], op0=ALU.mult, op1=ALU.mult), [bsrc, bst[col], bg], [u])
        bk = (bankB if back else bankF).next()
        for k in range(8):
            pe(lambda e, k=k, bk=bk: e.transpose(out=psb[:, bk, k * 128:(k + 1) * 128], in_=ubt[:, k * 128:(k + 1) * 128], identity=ident[:]), [u, bident], [bps[bk]], c=128)
        dve(lambda e, bk=bk: e.tensor_copy(out=dstT[:, :, t * 128:(t + 1) * 128], in_=psb[:, bk, :].rearrange("p (k t) -> p k t", k=8)), [bps[bk]], [bdst[t]])

    def proj_chunk(col0, n, ntile, uTb=None, buTb=None):
        uTb = uT if uTb is None else uTb
        buTb = buT if buTb is None else buTb
        bk = bankF.next()
        for k in range(8):
            pe(lambda e, k=k, bk=bk: e.matmul(ps[:, bk, 0:n], lhsT=Win[:, k, col0:col0 + 128], rhs=uTb[:, k, 0:n], start=(k == 0), stop=(k == 7)),
               [bWin[k]] + buTb[:ntile], [bps[bk]], c=n)
        return bk

    f1T_f = f1T.bitcast(F32)

    def carve(i):
        return f1T_f[:, 2 * i:2 * i + 2, :].rearrange("p a b -> p (a b)")

    class TSet:
        pass

    tsets = []
    for p_ in range(2):
        t_ = TSet()
        t_.T = [TT[p_][i][:, :] for i in range(5)]
        t_.bT = [[bTT[p_][i]] for i in range(5)]
        t_.xr, t_.bxr = xrs[p_][:, :], [bxrs[p_]]
        t_.xcb, t_.bxcb = xcbs[p_][:, :], [bxcbs[p_]]
        tsets.append(t_)
    for p_ in range(2):
        t_ = TSet()
        t_.T = [carve(5 * p_ + i) for i in range(5)]
        t_.bT = [[bf1T[2 * (5 * p_ + i)], bf1T[2 * (5 * p_ + i) + 1]] for i in range(5)]
        r0_ = 20 + 3 * p_
        t_.xr = f1T_f[:, r0_:r0_ + 3, :].rearrange("p a b -> p (a b)")[:, 0:G + 3]
        t_.bxr = [bf1T[r0_], bf1T[r0_ + 1], bf1T[r0_ + 2]]
        t_.xcb, t_.bxcb = f1T[:, 26 + p_, :], [bf1T[26 + p_]]
        tsets.append(t_)

    def lru_chunk(j, n, ntile, is_main, masked, ts, uTb=None, buTb=None, vmo=G):
        N = n
        uTb = uT if uTb is None else uTb
        buTb = buT if buTb is None else buTb
        T = ts.T
        bT = ts.bT
        xcb, bxcb = ts.xcb, ts.bxcb
        bk = proj_chunk(768 + 128 * j, n, ntile, uTb, buTb)
        xr_t = ts.xr
        bxr = ts.bxr
        def L(*xs):
            o = []
            for x in xs:
                if isinstance(x, list):
                    o.extend(x)
                else:
                    o.append(x)
            return o
        dve(lambda e: e.tensor_copy(out=xr_t[:, 0:3], in_=hist[:, j, :]), [bhist[j]], L(bxr))
        act(lambda e, bk=bk: e.copy(out=xr_t[:, 3:3 + N], in_=ps[:, bk, 0:N]), [bps[bk]], L(bxr))
        dve(lambda e: e.tensor_copy(out=hist[:, j, :], in_=xr_t[:, N:N + 3]), L(bxr), [bhist[j]])
        yield 1
        xc = T[0]
        dve(lambda e: e.tensor_scalar(out=xc[:, 0:N], in0=xr_t[:, 3:3 + N], scalar1=cw[:, 4 * j + 3:4 * j + 4], scalar2=cb[:, j:j + 1], op0=ALU.mult, op1=ALU.add), L(bxr, bconst), L(bT[0]))
        for tap in range(3):
            dve(lambda e, tap=tap: e.scalar_tensor_tensor(out=xc[:, 0:N], in0=xr_t[:, tap:tap + N], scalar=cw[:, 4 * j + tap:4 * j + tap + 1], in1=xc[:, 0:N], op0=ALU.mult, op1=ALU.add), L(bxr, bconst, bT[0]), L(bT[0]))
        dve(lambda e: e.tensor_copy(out=xcb[:, 0:N], in_=xc[:, 0:N]), L(bT[0]), L(bxcb))
        yield 1
        if is_main:
            yield 1
        bkr = bankF.next()
        pe(lambda e, bkr=bkr: e.matmul(ps[:, bkr, 0:N], lhsT=Wa[:, j, :], rhs=xcb[:, 0:N], start=True, stop=True), L(bWa, bxcb), [bps[bkr]], c=N)
        bki = bankF.next()
        pe(lambda e, bki=bki: e.matmul(ps[:, bki, 0:N], lhsT=Wx[:, j, :], rhs=xcb[:, 0:N], start=True, stop=True), L(bWx, bxcb), [bps[bki]], c=N)
        er, ei, a_, a2 = T[1], T[2], T[3], T[4]
        act(lambda e, bkr=bkr: e.activation(out=er[:, 0:N], in_=ps[:, bkr, 0:N], func=AF.Exp, scale=-1.0, bias=nba[:, j:j + 1]), [bps[bkr], bconst], L(bT[1]))
        act(lambda e, bki=bki: e.activation(out=ei[:, 0:N], in_=ps[:, bki, 0:N], func=AF.Exp, scale=-1.0, bias=nbx[:, j:j + 1]), [bps[bki], bconst], L(bT[2]))
        yield 1
        act(lambda e: e.activation(out=er[:, 0:N], in_=er[:, 0:N], func=AF.Ln, bias=1.0), L(bT[1]), L(bT[1]))
        act(lambda e: e.activation(out=ei[:, 0:N], in_=ei[:, 0:N], func=AF.Ln, bias=1.0), L(bT[2]), L(bT[2]))
        yield 1
        act(lambda e: e.activation(out=er[:, 0:N], in_=er[:, 0:N], func=AF.Exp, scale=-1.0), L(bT[1]), L(bT[1]))
        yield 1
        act(lambda e: e.activation(out=a2[:, 0:N], in_=er[:, 0:N], func=AF.Exp, scale=c2[:, j:j + 1]), L(bT[1], bconst), L(bT[4]))
        act(lambda e: e.activation(out=a_[:, 0:N], in_=er[:, 0:N], func=AF.Exp, scale=cneg[:, j:j + 1]), L(bT[1], bconst), L(bT[3]))
        yield 1
        act(lambda e: e.activation(out=a2[:, 0:N], in_=a2[:, 0:N], func=AF.Ln, scale=-1.0, bias=1.0), L(bT[4]), L(bT[4]))
        yield 1
        dve(lambda e: e.scalar_tensor_tensor(out=a2[:, 0:N], in0=a2[:, 0:N], scalar=0.5, in1=ei[:, 0:N], op0=ALU.mult, op1=ALU.subtract), L(bT[4], bT[2]), L(bT[4]))
        yield 1
        act(lambda e: e.activation(out=a2[:, 0:N], in_=a2[:, 0:N], func=AF.Exp), L(bT[4]), L(bT[4]))
        yield 1
        pool(lambda e: e.tensor_tensor(out=xc[:, 0:N], in0=a2[:, 0:N], in1=xc[:, 0:N], op=ALU.mult), L(bT[4], bT[0]), L(bT[0]))
        if masked:
            pool(lambda e: e.tensor_tensor(out=xc[:, 0:N], in0=xc[:, 0:N], in1=tmp2[:, vmo:vmo + N], op=ALU.mult), L(bT[0], bvmh[vmo // G]), L(bT[0]))
        yield 1
        h_ = T[2]
        dve(lambda e: e.tensor_tensor_scan(out=h_[:, 0:N], data0=a_[:, 0:N], data1=xc[:, 0:N], initial=hstate[:, j:j + 1], op0=ALU.mult, op1=ALU.add), L(bT[3], bT[0], bhst[j]), L(bT[2]))
        dve(lambda e: e.tensor_copy(out=hstate[:, j:j + 1], in_=h_[:, N - 1:N]), L(bT[2]), [bhst[j]])
        yield 1
        if is_main:
            bky = proj_chunk(1280 + 128 * j, n, ntile, uTb, buTb)
            yr, y2 = T[4], T[1]
            act(lambda e, bky=bky: e.copy(out=yr[:, 0:N], in_=ps[:, bky, 0:N]), [bps[bky]], L(bT[4]))
            act(lambda e, bky=bky: e.activation(out=y2[:, 0:N], in_=ps[:, bky, 0:N], func=AF.Square), [bps[bky]], L(bT[1]))
            yield 1
            dve(lambda e: e.tensor_scalar(out=y2[:, 0:N], in0=y2[:, 0:N], scalar1=0.044715, scalar2=1.0, op0=ALU.mult, op1=ALU.add), L(bT[1]), L(bT[1]))
            dve(lambda e: e.tensor_tensor(out=y2[:, 0:N], in0=y2[:, 0:N], in1=yr[:, 0:N], op=ALU.mult), L(bT[1], bT[4]), L(bT[1]))
            yield 1
            act(lambda e: e.activation(out=y2[:, 0:N], in_=y2[:, 0:N], func=AF.Exp, scale=-2.0 * GELU_C), L(bT[1]), L(bT[1]))
            yield 1
            act(lambda e: e.activation(out=y2[:, 0:N], in_=y2[:, 0:N], func=AF.Ln, bias=1.0), L(bT[1]), L(bT[1]))
            yield 1
            act(lambda e: e.activation(out=y2[:, 0:N], in_=y2[:, 0:N], func=AF.Exp, scale=-1.0), L(bT[1]), L(bT[1]))
            yield 1
            pool(lambda e: e.tensor_tensor(out=yr[:, 0:N], in0=yr[:, 0:N], in1=y2[:, 0:N], op=ALU.mult), L(bT[4], bT[1]), L(bT[4]))
            yield 1
            dve(lambda e: e.tensor_tensor(out=recT[:, j, 0:N], in0=yr[:, 0:N], in1=h_[:, 0:N], op=ALU.mult), L(bT[4], bT[2]), [brecT[j]])
            yield 1

    def kv_evac(ntile, blk0, halo, uTb=None, buTb=None):
        uTb = uT if uTb is None else uTb
        buTb = buT if buTb is None else buTb
        n = ntile * 128
        bk = proj_chunk(512, n, ntile, uTb, buTb)
        kbufs = bK[blk0:blk0 + ntile]
        act(lambda e, bk=bk: e.copy(out=KA[0:64, blk0 * 128:blk0 * 128 + n], in_=ps[0:64, bk, 0:n]), [bps[bk]], kbufs)
        act(lambda e, bk=bk: e.copy(out=KB[64:128, blk0 * 128:blk0 * 128 + n], in_=ps[64:128, bk, 0:n]), [bps[bk]], kbufs)
        bv = bankF.next()
        for t in range(ntile):
            for k in range(8):
                pe(lambda e, k=k, t=t, bv=bv: e.matmul(ps[:, bv, t * 128:(t + 1) * 128], lhsT=uTb[:, k, t * 128:(t + 1) * 128], rhs=Win[:, k, 640:768], start=(k == 0), stop=(k == 7)),
                   [bWin[k], buTb[t]], [bps[bv]], c=128)
        for t in range(ntile):
            src = ps[:, bv, t * 128:(t + 1) * 128].rearrange("p (a b) -> p a b", a=2)
            if halo:
                dve(lambda e, t=t, src=src: e.tensor_scalar(out=Vaug[:, blk0 + t, :, 0:64], in0=src, scalar1=vcol_s[:, 0:1], scalar2=None, op0=ALU.mult), [bps[bv], bconst], [bV[blk0 + t]])
                dve(lambda e, t=t: e.tensor_scalar(out=Vaug[:, blk0 + t, :, 64:65], in0=Vaug[:, blk0 + t, :, 64:65], scalar1=vcol_s[:, 0:1], scalar2=None, op0=ALU.mult), [bV[blk0 + t], bconst], [bV[blk0 + t]])
            else:
                dve(lambda e, t=t, src=src: e.tensor_copy(out=Vaug[:, blk0 + t, :, 0:64], in_=src), [bps[bv]], [bV[blk0 + t]])

    def rr(*gens):
        gens = [g for g in gens if g is not None]
        while gens:
            for g in list(gens):
                try:
                    next(g)
                    yield 1
                except StopIteration:
                    gens.remove(g)

    def rr_w(ga, na, gb):
        da = False
        db = gb is None
        while not (da and db):
            for _ in range(na):
                if da:
                    break
                try:
                    next(ga)
                    yield 1
                except StopIteration:
                    da = True
            if not db:
                try:
                    next(gb)
                    yield 1
                except StopIteration:
                    db = True

    def skewed(factories, D):
        active = []
        nxt = 0
        r = 0
        while nxt < len(factories) or active:
            if r % D == 0 and nxt < len(factories):
                active.append(factories[nxt]())
                nxt += 1
            for g_ in list(active):
                try:
                    next(g_)
                except StopIteration:
                    active.remove(g_)
            r += 1
            yield 1

    def seq(*gens):
        for g in gens:
            if g is not None:
                yield from g

    uTbufs = [(uT, buT), (u2T, bu2T)]

    def prefix_m1(pg):
        ntile = 4 if pg < 8 else 1
        tok0 = pg * G
        uTb, buTb = uTbufs[pg % 2]
        for t in range(ntile):
            r0 = tok0 + t * 128
            S.add("sp", lambda e, t=t, r0=r0: e.dma_start(out=xt[t % 2][:], in_=xp.ap()[r0:r0 + 128, :]), (), [bxt[t % 2]], dma_key=f"d_x{t % 2}")
            norm_transpose(xt[t % 2][:], bxt[t % 2], None, None, uTb, buTb, t, False)
            yield 1

    def run_prefix(pgs):
        D = 3
        ntl = lambda pg: 4 if pg < 8 else 1

        def vm_load(gi):
            pg = pgs[gi]
            n = ntl(pg) * 128
            off = (gi % 2) * G
            S.add("sp", lambda e: e.dma_start(out=tmp2[:, off:off + n], in_=vrow.ap()[0:1, pg * G:pg * G + n].partition_broadcast(128)), (), [bvmh[gi % 2]], dma_key=f"d_vm{gi % 2}")

        facts = []
        for gi, pg in enumerate(pgs):
            for j in range(4):
                i = 4 * gi + j
                uTb, buTb = uTbufs[pg % 2]
                facts.append(lambda j=j, pg=pg, i=i, uTb=uTb, buTb=buTb, gi=gi: lru_chunk(j, ntl(pg) * 128, ntl(pg), False, True, tsets[i % 4], uTb, buTb, (gi % 2) * G))
        m1g = {gi: prefix_m1(pg) for gi, pg in enumerate(pgs)}
        vm_load(0)
        for _ in m1g[0]:
            pass
        active = []
        nxt = 0
        r = 0
        while nxt < len(facts) or active:
            gi_n = (r + 4) // (4 * D)
            if 1 <= gi_n < len(pgs):
                t_ = (r + 4) - gi_n * 4 * D
                if t_ == 0:
                    vm_load(gi_n)
                if 0 <= t_ < 4:
                    try:
                        next(m1g[gi_n])
                    except StopIteration:
                        pass
            if r % D == 0 and nxt < len(facts):
                active.append(facts[nxt]())
                nxt += 1
            for g_ in list(active):
                try:
                    next(g_)
                except StopIteration:
                    active.remove(g_)
            r += 1
        if pgs[-1] == 8:
            uTb, buTb = uTbufs[0]
            kv_evac(1, 0, True, uTb, buTb)

    def front_a(g):
        for t in range(4):
            r0 = g * G + t * 128
            S.add("sp", lambda e, t=t, r0=r0: e.dma_start(out=xt[t % 2][:], in_=xm.ap()[r0:r0 + 128, :]), (), [bxt[t % 2]], dma_key=f"d_x{t % 2}")
            norm_transpose(xt[t % 2][:], bxt[t % 2], None, None, uT, buT, t, False)
            yield 1
        for c in range(4):
            bk = proj_chunk(128 * c, G, 4)
            act(lambda e, bk=bk, c=c: e.copy(out=qT[:, c, :], in_=ps[:, bk, :]), [bps[bk]], [bqT[c]])
            yield 1
        if g > 0:
            dve(lambda e: e.tensor_copy(out=KA[0:64, 0:128], in_=KA[0:64, 512:640]), [bK[4]], [bK[0]])
            dve(lambda e: e.tensor_copy(out=KB[64:128, 0:128], in_=KB[64:128, 512:640]), [bK[4]], [bK[0]])
            dve(lambda e: e.tensor_copy(out=Vaug[:, 0, :, :], in_=Vaug[:, 4, :, :]), [bV[4]], [bV[0]])
        kv_evac(4, 1, False)
        yield 1

    def front_b(g):
        lru = skewed([lambda j=j: lru_chunk(j, G, 4, True, False, tsets[j % 2]) for j in range(4)], 10)
        import os as _os
        if _os.environ.get("LRU_SEQ") == "1":
            for j in range(4):
                yield from lru_chunk(j, G, 4, True, False, tsets[j % 2])
            yield from attention(g)
        else:
            yield from rr(lru, attention(g))

    def attention(g):
        for n in range(4):
            b_o = [3, 4]
            def s_part(c):
                bs = bankF.next()
                pt = PT[c % 2]
                bpt = bPT[c % 2]
                qs = qT[:, c, n * 128:(n + 1) * 128]
                for a, Kt in enumerate((KA, KB)):
                    for pc in range(2):
                        blk = n + pc
                        col = (a * 2 + pc) * 128
                        pe(lambda e, Kt=Kt, blk=blk, col=col, bs=bs, qs=qs: e.matmul(ps[:, bs, col:col + 128], lhsT=Kt[:, blk * 128:(blk + 1) * 128], rhs=qs, start=True, stop=True),
                           [bK[blk], bqT[c]], [bps[bs]], c=128)
                for a in range(2):
                    h = c + 4 * a
                    act(lambda e, a=a, h=h, bs=bs, pt=pt: e.activation(out=pt[:, a * 256:(a + 1) * 256], in_=ps[:, bs, a * 256:(a + 1) * 256], func=AF.Exp, scale=0.125, bias=nsink[:, h:h + 1]),
                        [bps[bs], bconst], [bpt])
                dve(lambda e, pt=pt: e.tensor_tensor(out=pt[:], in0=pt[:], in1=maskPT[:], op=ALU.mult), [bpt, bmask], [bpt])

            def pv_part(c):
                pt = PT[c % 2]
                bpt = bPT[c % 2]
                for a in range(2):
                    h = c + 4 * a
                    bo = b_o[h // 4]
                    o0 = (h % 4) * 65
                    for pc in range(2):
                        blk = n + pc
                        col = (a * 2 + pc) * 128
                        pe(lambda e, a=a, blk=blk, col=col, bo=bo, o0=o0, pt=pt, pc=pc: e.matmul(ps[:, bo, o0:o0 + 65], lhsT=pt[:, col:col + 128], rhs=Vaug[:, blk, a, :], start=(pc == 0), stop=(pc == 1)),
                           [bpt, bV[blk]], [bps[bo]], c=65)

            s_part(0)
            yield 1
            for c in range(4):
                if c + 1 < 4:
                    s_part(c + 1)
                    yield 1
                pv_part(c)
                yield 1
            at = atok[n % 2]
            bat = batok[n % 2]
            for hh in range(2):
                bo = b_o[hh]
                ov = ps[:, bo, 0:260].rearrange("p (h d) -> p h d", h=4)
                dve(lambda e, ov=ov, hh=hh: e.tensor_scalar(out=den[:, hh * 4:hh * 4 + 4], in0=ov[:, :, 64], scalar1=1.0, scalar2=None, op0=ALU.add), [bps[bo]], [bden])
                dve(lambda e, hh=hh: e.reciprocal(out=den[:, hh * 4:hh * 4 + 4], in_=den[:, hh * 4:hh * 4 + 4]), [bden], [bden])
                for hq in range(4):
                    h = hh * 4 + hq
                    dve(lambda e, ov=ov, hq=hq, h=h, at=at: e.tensor_scalar(out=at[:, h * 64:(h + 1) * 64], in0=ov[:, hq, 0:64], scalar1=den[:, h:h + 1], scalar2=None, op0=ALU.mult), [bps[bo], bden], [bat])
            yield 1
            yield 1
            bt = bankF.next()
            for k in range(4):
                pe(lambda e, k=k, bt=bt, at=at: e.transpose(out=psb[:, bt, k * 128:(k + 1) * 128], in_=at[:, k * 128:(k + 1) * 128], identity=ident[:]), [bat, bident], [bps[bt]], c=128)
            act(lambda e, bt=bt, n=n: e.copy(out=attnT[:, :, n * 128:(n + 1) * 128], in_=psb[:, bt, 0:512].rearrange("p (k t) -> p k t", k=4)), [bps[bt]], [battnT[n]])
            yield 1

    def wload(slot, src_ap, view):
        return S.add("pool", lambda e: e.dma_start(out=view, in_=src_ap), (), [bwst[slot]], dma_key=f"d_w{slot}")

    w1v = w_ff1.ap().rearrange("(k p) n -> p k n", p=128)
    w2v = w_ff2.ap().rearrange("(m p) n -> p m n", p=128)
    wov = w_out.ap().rearrange("(k p) n -> p k n", p=128)

    def slot_w1(s):
        return wst[s][:].rearrange("p (k n) -> p k n", k=8)

    def slot_w2(s):
        return wst[s][:].rearrange("p (m n) -> p m n", m=32)

    def back_m5(g):
        if g == 0:
            for hf in range(2):
                wload(hf, wov[:, :, hf * 512:(hf + 1) * 512], slot_w1(hf))
        for t in range(4):
            r0 = g * G + t * 128
            S.add("sp", lambda e, t=t, r0=r0: e.dma_start(out=h1[:, t, :], in_=xm.ap()[r0:r0 + 128, :]), (), [bh1[t]], dma_key=f"d_h{t}")
            for hf in range(2):
                bk = bankB.next()
                for k in range(8):
                    src = attnT[:, k, t * 128:(t + 1) * 128] if k < 4 else recT[:, k - 4, t * 128:(t + 1) * 128]
                    bsrc = battnT[t] if k < 4 else brecT[k - 4]
                    pe(lambda e, k=k, bk=bk, src=src, hf=hf: e.matmul(ps[:, bk, :], lhsT=src, rhs=slot_w1(hf)[:, k, :], start=(k == 0), stop=(k == 7)),
                       [bsrc, bwst[hf]], [bps[bk]])
                act(lambda e, bk=bk, t=t, hf=hf: e.copy(out=ftok[:, t, hf * 512:(hf + 1) * 512], in_=ps[:, bk, :]), [bps[bk]], [bftok[t]])
            yield 8200
        wload(0, w1v[:, :, 0:512], slot_w1(0))
        wload(1, w1v[:, :, 512:1024], slot_w1(1))
        state["allow_fb"] = True
        for t in range(4):
            col = statB.next()
            act(lambda e, t=t, col=col: e.activation(out=ub[2][:], in_=ftok[:, t, :], func=AF.Square, scale=1.0 / 32.0, accum_out=st_ms[:, col:col + 1]), [bftok[t]], [bub[2], bst[col]])
            rstd_from_ms(col)
            dve(lambda e, t=t, col=col: e.scalar_tensor_tensor(out=ftok[:, t, :], in0=ftok[:, t, :], scalar=st_rs[:, col:col + 1], in1=gt2[:], op0=ALU.mult, op1=ALU.mult), [bftok[t], bst[col], bgt2], [bftok[t]])
            dve(lambda e, t=t: e.tensor_tensor(out=h1[:, t, :], in0=h1[:, t, :], in1=ftok[:, t, :], op=ALU.add), [bh1[t], bftok[t]], [bh1[t]])
            norm_transpose(h1[:, t, :], bh1[t], gt3, bgt3, u2T, bu2T, t, True)
            yield 12300

    def back_ffn(g):
        for fg in range(8):
            s = fg % 2
            for m in range(4):
                bk = bankB.next()
                mm = fg * 4 + m
                for k in range(8):
                    pe(lambda e, k=k, bk=bk, s=s, m=m: e.matmul(ps[:, bk, :], lhsT=slot_w1(s)[:, k, m * 128:(m + 1) * 128], rhs=u2T[:, k, :], start=(k == 0), stop=(k == 7)),
                       [bwst[s]] + bu2T, [bps[bk]])
                act(lambda e, bk=bk, mm=mm: e.activation(out=f1T[:, mm, :], in_=ps[:, bk, :], func=AF.Relu), [bps[bk]], [bf1T[mm]])
                dve(lambda e, mm=mm: e.tensor_tensor(out=f1T[:, mm, :], in0=f1T[:, mm, :], in1=f1T[:, mm, :], op=ALU.mult), [bf1T[mm]], [bf1T[mm]])
                yield 900
            nxt = fg + 2
            if nxt < 8:
                wload(s, w1v[:, :, nxt * 512:(nxt + 1) * 512], slot_w1(s))
            else:
                oc = nxt - 8
                wload(s, w2v[:, :, oc * 128:(oc + 1) * 128], slot_w2(s))
        for oc in range(8):
            s = oc % 2
            bk = bankB.next()
            for m in range(32):
                pe(lambda e, m=m, bk=bk, s=s: e.matmul(ps[:, bk, :], lhsT=slot_w2(s)[:, m, :], rhs=f1T[:, m, :], start=(m == 0), stop=(m == 31)),
                   [bwst[s], bf1T[m]], [bps[bk]])
                if m % 8 == 7:
                    yield 900
            if oc + 2 < 8:
                wload(s, w2v[:, :, (oc + 2) * 128:(oc + 3) * 128], slot_w2(s))
            act(lambda e, bk=bk: e.copy(out=fTc, in_=ps[:, bk, :]), [bps[bk]], [bfTc])
            bt = bankB.next()
            for t in range(4):
                pe(lambda e, t=t, bt=bt: e.transpose(out=ps[:, bt, t * 128:(t + 1) * 128], in_=tmp2[:, t * 128:(t + 1) * 128], identity=ident_f[:]), [bfTc, bident], [bps[bt]], c=128)
            dve(lambda e, bt=bt, oc=oc: e.tensor_copy(out=ftok[:, :, oc * 128:(oc + 1) * 128], in_=ps[:, bt, :].rearrange("p (t f) -> p t f", t=4)), [bps[bt]], bftok)
            yield 900
        if g + 1 < n_main_groups:
            for hf in range(2):
                wload(hf, wov[:, :, hf * 512:(hf + 1) * 512], slot_w1(hf))

    def back_final(g):
        for t in range(4):
            col = statB.next()
            act(lambda e, t=t, col=col: e.activation(out=ub[2][:], in_=ftok[:, t, :], func=AF.Square, scale=1.0 / 32.0, accum_out=st_ms[:, col:col + 1]), [bftok[t]], [bub[2], bst[col]])
            rstd_from_ms(col)
            dve(lambda e, t=t, col=col: e.scalar_tensor_tensor(out=ftok[:, t, :], in0=ftok[:, t, :], scalar=st_rs[:, col:col + 1], in1=gt4[:], op0=ALU.mult, op1=ALU.mult), [bftok[t], bst[col], bgt4], [bftok[t]])
            dve(lambda e, t=t: e.tensor_tensor(out=h1[:, t, :], in0=h1[:, t, :], in1=ftok[:, t, :], op=ALU.add), [bh1[t], bftok[t]], [bh1[t]])
            r0 = g * G + t * 128
            out_ops.append(S.add("sp", lambda e, t=t, r0=r0: e.dma_start(out=out_d.ap()[r0:r0 + 128, :], in_=h1[:, t, :]), [bh1[t]], (), dma_key=f"d_o{t}"))
            yield 1200

    def run_alone(gen, who="A"):
        state["cur"] = who
        for _ in gen:
            pass

    def merge(genA, totA, genB, totB):
        a0, b0 = cost["A"], cost["B"]
        doneA = genA is None
        doneB = genB is None
        while not (doneA and doneB):
            pa = (cost["A"] - a0) / totA
            pb = (cost["B"] - b0) / totB
            if (not doneA) and (doneB or pa <= pb):
                state["cur"] = "A"
                try:
                    next(genA)
                except StopIteration:
                    doneA = True
            else:
                state["cur"] = "B"
                try:
                    next(genB)
                except StopIteration:
                    doneB = True

    def zip4(ga, gb):
        if ga is not None:
            for _ in range(4):
                yield next(ga)
                yield next(gb)
            for v in ga:
                yield v
        yield from gb

    def merge2(A_gens, genB):
        ai = 0
        budget = 0.0
        state["allow_fb"] = False
        while True:
            state["cur"] = "B"
            try:
                budget += next(genB)
            except StopIteration:
                break
            state["cur"] = "A"
            while budget > 0 and ai < len(A_gens):
                if ai >= 1 and not state["allow_fb"]:
                    break
                c0 = cost["A"]
                try:
                    next(A_gens[ai])
                except StopIteration:
                    ai += 1
                    continue
                budget -= max(cost["A"] - c0, 700.0)
        state["cur"] = "A"
        while ai < len(A_gens):
            for _ in A_gens[ai]:
                pass
            ai += 1

    setup()
    pgs = list(range(9 - n_pre_groups, 9))
    state["cur"] = "A"
    run_prefix(pgs)
    if pipelined:
        run_alone(front_a(0))
        run_alone(front_b(0))
        for g in range(n_main_groups):
            nxt = g + 1 < n_main_groups
            Bs = seq(zip4(back_final(g - 1) if g > 0 else None, back_m5(g)), back_ffn(g))
            merge2([front_a(g + 1), front_b(g + 1)] if nxt else [], Bs)
        run_alone(back_final(n_main_groups - 1), "B")
    else:
        for g in range(n_main_groups):
            run_alone(front_a(g))
            run_alone(front_b(g))
            run_alone(back_m5(g), "B")
            run_alone(back_ffn(g), "B")
            run_alone(back_final(g), "B")
    S.add("sp", None, extra_deps=out_ops)
    print("sbuf bytes remaining", nc.sbuf_bytes_remaining)
    S.emit(nc)
    es.close()
    return nc


_NC_CACHE = {}


def _prep_inputs(x, meta_tokens, g_pre_mix, w_in, conv_w, conv_b, w_a, b_a, w_x, b_x,
                 lru_lambda, attn_sinks, w_out, g_post_mix, g_pre_ffn, w_ff1, w_ff2, g_post_ffn):
    f = np.float32
    x = np.asarray(x, f)
    meta = np.asarray(meta_tokens, f)
    w_in0 = np.asarray(w_in, f)[0]
    order = []
    for c in range(4):
        order += list(range(c * 64, c * 64 + 64)) + list(range((4 + c) * 64, (4 + c) * 64 + 64))
    perm = np.array(order + list(range(512, 1792)))
    w_in_p = np.ascontiguousarray(w_in0[:, perm])

    def bd(w):
        w = np.asarray(w, f)[0]
        o = np.zeros((4, 128, 128), f)
        for j in range(4):
            o[j, 0:64, 0:64] = w[2 * j]
            o[j, 64:128, 64:128] = w[2 * j + 1]
        return o

    def chan(v):
        return np.ascontiguousarray(np.asarray(v, f).reshape(4, 128).T)

    cw = np.asarray(conv_w, f)[0]
    cw_l = np.ascontiguousarray(cw.reshape(4, 4, 128).transpose(2, 1, 0).reshape(128, 16))
    shared = {
        "w_in": w_in_p,
        "w_out": np.ascontiguousarray(np.asarray(w_out, f)[0]),
        "w_ff1": np.ascontiguousarray(np.asarray(w_ff1, f)[0]),
        "w_ff2": np.ascontiguousarray(np.asarray(w_ff2, f)[0]),
        "wa_bd": bd(w_a),
        "wx_bd": bd(w_x),
        "cw": cw_l,
        "cb": chan(np.asarray(conv_b, f)[0]),
        "ba": chan(np.asarray(b_a, f)[0]),
        "bx": chan(np.asarray(b_x, f)[0]),
        "lam": chan(np.asarray(lru_lambda, f)[0]),
        "sinks": np.ascontiguousarray(np.asarray(attn_sinks, f)[0].reshape(1, 8)),
        "g1c": np.ascontiguousarray(np.asarray(g_pre_mix, f)[0].reshape(8, 128).T),
        "g2": np.ascontiguousarray(np.asarray(g_post_mix, f)[0].reshape(1, D)),
        "g3": np.ascontiguousarray(np.asarray(g_pre_ffn, f)[0].reshape(1, D)),
        "g4": np.ascontiguousarray(np.asarray(g_post_ffn, f)[0].reshape(1, D)),
    }
    in_maps = []
    for c in range(8):
        b, hf = c // 2, c % 2
        xp = np.zeros((NPRE, D), f)
        vrow = np.zeros((1, NPRE), f)
        vcol = np.zeros((128, 1), f)
        if hf == 0:
            xp[NPRE - 16:] = meta
            vrow[0, NPRE - 16:] = 1.0
            vcol[112:, 0] = 1.0
        else:
            xp[112:128] = meta
            xp[128:] = x[b, 0:4096]
            vrow[0, 112:] = 1.0
            vcol[:, 0] = 1.0
        m = dict(shared)
        m["xp"] = xp
        m["xm"] = np.ascontiguousarray(x[b, hf * 4096:(hf + 1) * 4096])
        m["vrow"] = vrow
        m["vcol"] = vcol
        in_maps.append(m)
    return in_maps


def kernel(**inputs):
    in_maps = _prep_inputs(**inputs)
    if "nc" not in _NC_CACHE:
        _NC_CACHE["nc"] = build_program()
    nc = _NC_CACHE["nc"]
    res = run_bass_kernel_spmd(nc, in_maps, core_ids=list(range(8)))
    out = np.empty((4, 8192, D), np.float32)
    for c in range(8):
        b, hf = c // 2, c % 2
        out[b, hf * 4096:(hf + 1) * 4096] = res.results[c]["out"]
    return out
```

```python
from contextlib import ExitStack

import numpy as np
import concourse.bass as bass
import concourse.mybir as mybir
from concourse.bass_utils import run_bass_kernel_spmd

F32 = mybir.dt.float32
BF16 = mybir.dt.bfloat16
AF = mybir.ActivationFunctionType
ALU = mybir.AluOpType

D = 1024
NT_MAIN = 32
NPRE = 4224
G = 512
EPS = 1e-6
GELU_C = 0.7978845608028654


class Buf:
    __slots__ = ("name", "last_write", "readers")

    def __init__(self, name):
        self.name = name
        self.last_write = None
        self.readers = {}


class Op:
    __slots__ = ("eng", "fn", "deps", "signals", "key", "tick", "is_dma", "idx")

    def __init__(self, eng, fn, key, is_dma):
        self.eng = eng
        self.fn = fn
        self.key = key
        self.is_dma = is_dma
        self.deps = []
        self.signals = False
        self.tick = 0


ENGS = ("sp", "act", "dve", "pool", "pe")


class Sched:
    def __init__(self):
        self.queues = {e: [] for e in ENGS}
        self.nops = 0

    def add(self, eng, fn, reads=(), writes=(), dma_key=None, extra_deps=()):
        is_dma = dma_key is not None
        key = dma_key if is_dma else eng
        op = Op(eng, fn, key, is_dma)
        op.idx = self.nops
        self.nops += 1
        deps = {}
        for b in reads:
            w = b.last_write
            if w is not None:
                deps[id(w)] = w
        for b in writes:
            w = b.last_write
            if w is not None:
                deps[id(w)] = w
            for r in b.readers.values():
                deps[id(r)] = r
        for d in extra_deps:
            deps[id(d)] = d
        out = []
        for d in deps.values():
            if d is op:
                continue
            if d.key == "pe" and key == "pe":
                continue
            out.append(d)
            d.signals = True
        op.deps = out
        for b in reads:
            b.readers[key] = op
        for b in writes:
            b.last_write = op
            b.readers = {}
        self.queues[eng].append(op)
        return op

    def emit(self, nc):
        counts = {}
        keys = []
        allops = []
        for e in ENGS:
            allops.extend(self.queues[e])
        allops.sort(key=lambda o: o.idx)
        for op in allops:
            if op.key not in counts:
                counts[op.key] = 0
                keys.append(op.key)
            if op.signals:
                counts[op.key] += 16 if op.is_dma else 1
            op.tick = counts[op.key]
        with ExitStack() as es:
            sems = {k: es.enter_context(nc.semaphore("s_" + k)) for k in keys}
            with nc.Block() as block:
                def run(engname, eng):
                    waited = {}
                    for op in self.queues[engname]:
                        need = {}
                        for d in op.deps:
                            if d.tick > need.get(d.key, 0):
                                need[d.key] = d.tick
                        for k, v in need.items():
                            if waited.get(k, 0) < v:
                                eng.wait_ge(sems[k], v)
                                waited[k] = v
                        if op.fn is not None:
                            ins = op.fn(eng)
                            if op.signals:
                                ins.then_inc(sems[op.key], 16 if op.is_dma else 1)

                @block.sync
                def _(eng):
                    run("sp", eng)

                @block.scalar
                def _(eng):
                    run("act", eng)

                @block.vector
                def _(eng):
                    run("dve", eng)

                @block.gpsimd
                def _(eng):
                    run("pool", eng)

                @block.tensor
                def _(eng):
                    run("pe", eng)


def build_program(n_main_groups=8, n_pre_groups=9, pipelined=True):
    nc = bass.Bass("TRN2", target_bir_lowering=False)
    S = Sched()

    def din(name, shape):
        return nc.dram_tensor(name, list(shape), F32, kind="ExternalInput")

    xp = din("xp", [NPRE, D])
    xm = din("xm", [NT_MAIN * 128, D])
    vrow = din("vrow", [1, NPRE])
    vcol = din("vcol", [128, 1])
    w_in = din("w_in", [D, 1792])
    w_out = din("w_out", [D, D])
    w_ff1 = din("w_ff1", [8, 128, 4096])
    w_ff2 = din("w_ff2", [8, 128, 4096])
    wa_bd = din("wa_bd", [4, 128, 128])
    wx_bd = din("wx_bd", [4, 128, 128])
    cw_d = din("cw", [128, 16])
    cb_d = din("cb", [128, 4])
    ba_d = din("ba", [128, 4])
    bx_d = din("bx", [128, 4])
    lam_d = din("lam", [128, 4])
    sink_d = din("sinks", [1, 8])
    g1c_d = din("g1c", [128, 8])
    g2_d = din("g2", [1, D])
    g3_d = din("g3", [1, D])
    g4_d = din("g4", [1, D])
    out_d = nc.dram_tensor("out", [NT_MAIN * 128, D], F32, kind="ExternalOutput")

    es = ExitStack()

    def sb(name, shape, dt=F32):
        return es.enter_context(nc.sbuf_tensor("sb_" + name, list(shape), dt))

    def B(name):
        return Buf(name)

    Win = sb("Win", [128, 8, 1792], BF16)
    Wa = sb("Wa", [128, 4, 128], BF16)
    Wx = sb("Wx", [128, 4, 128], BF16)
    gt2 = sb("gt2", [128, D])
    gt3 = sb("gt3", [128, D])
    gt4 = sb("gt4", [128, D])
    wst = [sb(f"wst{i}", [128, 4096], BF16) for i in range(2)]
    xt = [sb(f"xt{i}", [128, D]) for i in range(2)]
    h1 = sb("h1", [128, 4, D])
    ub = [sb(f"ub{i}", [128, D], BF16) for i in range(3)]
    uT = sb("uT", [128, 8, G], BF16)
    u2T = sb("u2T", [128, 8, G], BF16)
    qT = sb("qT", [128, 4, G], BF16)
    KA = sb("KA", [128, 640], BF16)
    KB = sb("KB", [128, 640], BF16)
    Vaug = sb("Vaug", [128, 5, 2, 65], BF16)
    PT = [sb(f"PT{i}", [128, 512], BF16) for i in range(2)]
    maskPT = sb("maskPT", [128, 512], BF16)
    atok = [sb(f"atok{i}", [128, 512], BF16) for i in range(2)]
    attnT = sb("attnT", [128, 4, G], BF16)
    recT = sb("recT", [128, 4, G], BF16)
    xrs = [sb(f"xrs{i}", [128, G + 3]) for i in range(2)]
    TT = [[sb(f"T{p}_{i}", [128, G]) for i in range(5)] for p in range(2)]
    xcbs = [sb(f"xcb{p}", [128, G], BF16) for p in range(2)]
    hist = sb("hist", [128, 4, 3])
    hstate = sb("hstate", [128, 4])
    ftok = sb("ftok", [128, 4, D])
    tmp2 = sb("tmp2", [128, D])
    f1T = sb("f1T", [128, 32, G], BF16)
    fTc = tmp2[:, 0:G]
    vm = tmp2[:, G:2 * G]
    cw = sb("cw", [128, 16])
    cb = sb("cb", [128, 4])
    nba = sb("nba", [128, 4])
    nbx = sb("nbx", [128, 4])
    cneg = sb("cneg", [128, 4])
    c2 = sb("c2", [128, 4])
    nsink = sb("nsink", [128, 8])
    g1c = sb("g1c", [128, 8])
    vcol_s = sb("vcol_s", [128, 1])
    ident_f = sb("ident_f", [128, 128])
    ident = sb("ident", [128, 128], BF16)
    ones_f = sb("ones_f", [128, 128])
    mtmp = sb("mtmp", [128, 128])
    st_ms = sb("st_ms", [128, 8])
    st_rs = sb("st_rs", [128, 8])
    den = sb("den", [128, 8])
    ps = es.enter_context(nc.psum_tensor("ps", [128, 8, 512], F32))
    psb = ps.bitcast(BF16)

    bWin = [B(f"Win{k}") for k in range(8)]
    bWa, bWx = B("Wa"), B("Wx")
    bgt2, bgt3, bgt4 = B("gt2"), B("gt3"), B("gt4")
    bwst = [B("wst0"), B("wst1")]
    bxt = [B("xt0"), B("xt1")]
    bh1 = [B(f"h1_{t}") for t in range(4)]
    bub = [B(f"ub{i}") for i in range(3)]
    buT = [B(f"uT{t}") for t in range(4)]
    bu2T = [B(f"u2T{t}") for t in range(4)]
    bqT = [B(f"qT{c}") for c in range(4)]
    bK = [B(f"K{b}") for b in range(5)]
    bV = [B(f"V{b}") for b in range(5)]
    bPT = [B("PT0"), B("PT1")]
    bmask = B("maskPT")
    batok = [B("atok0"), B("atok1")]
    battnT = [B(f"attnT{t}") for t in range(4)]
    brecT = [B(f"recT{j}") for j in range(4)]
    bxrs = [B("xrs0"), B("xrs1")]
    bTT = [[B(f"T{p}_{i}") for i in range(5)] for p in range(2)]
    bxcbs = [B("xcb0"), B("xcb1")]
    bhist = [B(f"hist{j}") for j in range(4)]
    bhst = [B(f"hst{j}") for j in range(4)]
    bftok = [B(f"ftok{t}") for t in range(4)]
    btmp2 = B("tmp2")
    bf1T = [B(f"f1T{m}") for m in range(32)]
    bfTc = btmp2
    bvm = btmp2
    bvmh = [B("vmh0"), B("vmh1")]
    bconst = B("const")
    bident = B("ident")
    bst = [B(f"st{i}") for i in range(8)]
    bden = B("den")
    bps = [B(f"ps{i}") for i in range(8)]

    class Rot:
        def __init__(self, ids):
            self.ids = list(ids)
            self.i = 0

        def next(self):
            v = self.ids[self.i % len(self.ids)]
            self.i += 1
            return v

    bankF = Rot([0, 1, 2])
    bankB = Rot([5, 6, 7])
    statF = Rot([0, 1, 2, 3])
    statB = Rot([4, 5, 6, 7])
    out_ops = []
    cost = {"A": 0.0, "B": 0.0}
    state = {"cur": "A"}

    def act(fn, reads, writes):
        return S.add("act", fn, reads, writes)

    def dve(fn, reads, writes):
        return S.add("dve", fn, reads, writes)

    def pool(fn, reads, writes):
        return S.add("pool", fn, reads, writes)

    def pe(fn, reads, writes, c=512):
        cost[state["cur"]] += max(c, 64)
        return S.add("pe", fn, reads, writes)

    def setup():
        sp = lambda fn, w, key: S.add("sp", fn, (), w, dma_key=key)
        sp(lambda e: e.dma_start(out=cw[:], in_=cw_d.ap()), [bconst], "d_c0")
        sp(lambda e: e.dma_start(out=cb[:], in_=cb_d.ap()), [bconst], "d_c0")
        sp(lambda e: e.dma_start(out=nba[:], in_=ba_d.ap()), [bconst], "d_c0")
        sp(lambda e: e.dma_start(out=nbx[:], in_=bx_d.ap()), [bconst], "d_c0")
        sp(lambda e: e.dma_start(out=cneg[:], in_=lam_d.ap()), [bconst], "d_c0")
        sp(lambda e: e.dma_start(out=nsink[:], in_=sink_d.ap().partition_broadcast(128)), [bconst], "d_c0")
        sp(lambda e: e.dma_start(out=g1c[:], in_=g1c_d.ap()), [bconst], "d_c0")
        sp(lambda e: e.dma_start(out=vcol_s[:], in_=vcol.ap()), [bconst], "d_c0")
        sp(lambda e: e.dma_start(out=gt2[:], in_=g2_d.ap().partition_broadcast(128)), [bgt2], "d_c1")
        sp(lambda e: e.dma_start(out=gt3[:], in_=g3_d.ap().partition_broadcast(128)), [bgt3], "d_c2")
        sp(lambda e: e.dma_start(out=gt4[:], in_=g4_d.ap().partition_broadcast(128)), [bgt4], "d_c3")
        S.add("pool", lambda e: e.dma_start(out=Wa[:], in_=wa_bd.ap().rearrange("j p m -> p j m")), (), [bWa], dma_key="d_wa")
        S.add("pool", lambda e: e.dma_start(out=Wx[:], in_=wx_bd.ap().rearrange("j p m -> p j m")), (), [bWx], dma_key="d_wx")
        win_v = w_in.ap().rearrange("(k p) n -> p k n", p=128)
        stage = ftok
        for k in range(8):
            t = k % 4
            S.add("sp", lambda e, k=k, t=t: e.dma_start(out=stage[:, t, :], in_=win_v[:, k, 0:1024]), (), [bftok[t]], dma_key=f"d_st{t}")
            dve(lambda e, k=k, t=t: e.tensor_scalar(out=Win[:, k, 0:1024], in0=stage[:, t, :], scalar1=g1c[:, k:k + 1], scalar2=None, op0=ALU.mult), [bftok[t], bconst], [bWin[k]])
            S.add("sp", lambda e, k=k, t=t: e.dma_start(out=stage[:, t, 0:768], in_=win_v[:, k, 1024:1792]), (), [bftok[t]], dma_key=f"d_st{t}")
            dve(lambda e, k=k, t=t: e.tensor_scalar(out=Win[:, k, 1024:1792], in0=stage[:, t, 0:768], scalar1=g1c[:, k:k + 1], scalar2=None, op0=ALU.mult), [bftok[t], bconst], [bWin[k]])
        pool(lambda e: e.memset(ones_f[:], 1.0), [], [bident])
        pool(lambda e: e.affine_select(out=ident_f[:], in_=ones_f[:], pattern=[[-1, 128]], compare_op=ALU.is_equal, fill=0.0, base=0, channel_multiplier=1), [bident], [bident])
        pool(lambda e: e.tensor_copy(out=ident[:], in_=ident_f[:]), [bident], [bident])
        mview = maskPT[:].rearrange("p (a b c) -> p a b c", a=2, b=2)
        pool(lambda e: e.affine_select(out=mtmp[:], in_=ones_f[:], pattern=[[-1, 128]], compare_op=ALU.is_ge, fill=0.0, base=-1, channel_multiplier=1), [bident], [bmask])
        for a in range(2):
            pool(lambda e, a=a: e.tensor_copy(out=mview[:, a, 0, :], in_=mtmp[:]), [bmask], [bmask])
        pool(lambda e: e.affine_select(out=mtmp[:], in_=ones_f[:], pattern=[[1, 128]], compare_op=ALU.is_ge, fill=0.0, base=0, channel_multiplier=-1), [bident, bmask], [bmask])
        for a in range(2):
            pool(lambda e, a=a: e.tensor_copy(out=mview[:, a, 1, :], in_=mtmp[:]), [bmask], [bmask])
        pool(lambda e: e.memset(KA[:], 0.0), [], bK)
        pool(lambda e: e.memset(KB[:], 0.0), [], bK)
        pool(lambda e: e.memset(Vaug[:].rearrange("p a b c -> p (a b c)"), 1.0), [], bV)
        pool(lambda e: e.memset(hist[:].rearrange("p a b -> p (a b)"), 0.0), [], bhist)
        pool(lambda e: e.memset(hstate[:], 0.0), [], bhst)
        act(lambda e: e.activation(out=cneg[:], in_=cneg[:], func=AF.Exp, scale=-1.0), [bconst], [bconst])
        act(lambda e: e.activation(out=cneg[:], in_=cneg[:], func=AF.Ln, bias=1.0), [bconst], [bconst])
        dve(lambda e: e.tensor_scalar(out=c2[:], in0=cneg[:], scalar1=-16.0, scalar2=None, op0=ALU.mult), [bconst], [bconst])
        dve(lambda e: e.tensor_scalar(out=cneg[:], in0=cneg[:], scalar1=-8.0, scalar2=None, op0=ALU.mult), [bconst], [bconst])
        dve(lambda e: e.tensor_scalar(out=nba[:], in0=nba[:], scalar1=-1.0, scalar2=None, op0=ALU.mult), [bconst], [bconst])
        dve(lambda e: e.tensor_scalar(out=nbx[:], in0=nbx[:], scalar1=-1.0, scalar2=None, op0=ALU.mult), [bconst], [bconst])
        dve(lambda e: e.tensor_scalar(out=nsink[:], in0=nsink[:], scalar1=-1.0, scalar2=None, op0=ALU.mult), [bconst], [bconst])

    def rstd_from_ms(col):
        act(lambda e: e.activation(out=st_rs[:, col:col + 1], in_=st_ms[:, col:col + 1], func=AF.Ln, bias=EPS), [bst[col]], [bst[col]])
        act(lambda e: e.activation(out=st_rs[:, col:col + 1], in_=st_rs[:, col:col + 1], func=AF.Exp, scale=-0.5), [bst[col]], [bst[col]])

    def norm_transpose(src_ap, bsrc, gtile, bg, dstT, bdst, t, back):
        col = (statB if back else statF).next()
        ui = 2 if back else (t % 2)
        u = bub[ui]
        ubt = ub[ui]
        act(lambda e: e.activation(out=ubt[:], in_=src_ap, func=AF.Square, scale=1.0 / 32.0, accum_out=st_ms[:, col:col + 1]), [bsrc], [u, bst[col]])
        rstd_from_ms(col)
        if gtile is None:
            dve(lambda e: e.tensor_scalar(out=ubt[:], in0=src_ap, scalar1=st_rs[:, col:col + 1], scalar2=None, op0=ALU.mult), [bsrc, bst[col]], [u])
        else:
            dve(lambda e: e.scalar_tensor_tensor(out=ubt[:], in0=src_ap, scalar=st_rs[:, col:col + 1], in1=gtile[:], op0=ALU.mult, op1=ALU.mult), [bsrc, bst[col], bg], [u])
        bk = (bankB if back else bankF).next()
        for k in range(8):
            pe(lambda e, k=k, bk=bk: e.transpose(out=psb[:, bk, k * 128:(k + 1) * 128], in_=ubt[:, k * 128:(k + 1) * 128], identity=ident[:]), [u, bident], [bps[bk]], c=128)
        dve(lambda e, bk=bk: e.tensor_copy(out=dstT[:, :, t * 128:(t + 1) * 128], in_=psb[:, bk, :].rearrange("p (k t) -> p k t", k=8)), [bps[bk]], [bdst[t]])

    def proj_chunk(col0, n, ntile, uTb=None, buTb=None):
        uTb = uT if uTb is None else uTb
        buTb = buT if buTb is None else buTb
        bk = bankF.next()
        for k in range(8):
            pe(lambda e, k=k, bk=bk: e.matmul(ps[:, bk, 0:n], lhsT=Win[:, k, col0:col0 + 128], rhs=uTb[:, k, 0:n], start=(k == 0), stop=(k == 7)),
               [bWin[k]] + buTb[:ntile], [bps[bk]], c=n)
        return bk

    f1T_f = f1T.bitcast(F32)

    def carve(i):
        return f1T_f[:, 2 * i:2 * i + 2, :].rearrange("p a b -> p (a b)")

    class TSet:
        pass

    tsets = []
    for p_ in range(2):
        t_ = TSet()
        t_.T = [TT[p_][i][:, :] for i in range(5)]
        t_.bT = [[bTT[p_][i]] for i in range(5)]
        t_.xr, t_.bxr = xrs[p_][:, :], [bxrs[p_]]
        t_.xcb, t_.bxcb = xcbs[p_][:, :], [bxcbs[p_]]
        tsets.append(t_)
    for p_ in range(2):
        t_ = TSet()
        t_.T = [carve(5 * p_ + i) for i in range(5)]
        t_.bT = [[bf1T[2 * (5 * p_ + i)], bf1T[2 * (5 * p_ + i) + 1]] for i in range(5)]
        r0_ = 20 + 3 * p_
        t_.xr = f1T_f[:, r0_:r0_ + 3, :].rearrange("p a b -> p (a b)")[:, 0:G + 3]
        t_.bxr = [bf1T[r0_], bf1T[r0_ + 1], bf1T[r0_ + 2]]
        t_.xcb, t_.bxcb = f1T[:, 26 + p_, :], [bf1T[26 + p_]]
        tsets.append(t_)

    def lru_chunk(j, n, ntile, is_main, masked, ts, uTb=None, buTb=None, vmo=G):
        N = n
        uTb = uT if uTb is None else uTb
        buTb = buT if buTb is None else buTb
        T = ts.T
        bT = ts.bT
        xcb, bxcb = ts.xcb, ts.bxcb
        bk = proj_chunk(768 + 128 * j, n, ntile, uTb, buTb)
        xr_t = ts.xr
        bxr = ts.bxr
        def L(*xs):
            o = []
            for x in xs:
                if isinstance(x, list):
                    o.extend(x)
                else:
                    o.append(x)
            return o
        dve(lambda e: e.tensor_copy(out=xr_t[:, 0:3], in_=hist[:, j, :]), [bhist[j]], L(bxr))
        act(lambda e, bk=bk: e.copy(out=xr_t[:, 3:3 + N], in_=ps[:, bk, 0:N]), [bps[bk]], L(bxr))
        dve(lambda e: e.tensor_copy(out=hist[:, j, :], in_=xr_t[:, N:N + 3]), L(bxr), [bhist[j]])
        yield 1
        xc = T[0]
        dve(lambda e: e.tensor_scalar(out=xc[:, 0:N], in0=xr_t[:, 3:3 + N], scalar1=cw[:, 4 * j + 3:4 * j + 4], scalar2=cb[:, j:j + 1], op0=ALU.mult, op1=ALU.add), L(bxr, bconst), L(bT[0]))
        for tap in range(3):
            dve(lambda e, tap=tap: e.scalar_tensor_tensor(out=xc[:, 0:N], in0=xr_t[:, tap:tap + N], scalar=cw[:, 4 * j + tap:4 * j + tap + 1], in1=xc[:, 0:N], op0=ALU.mult, op1=ALU.add), L(bxr, bconst, bT[0]), L(bT[0]))
        dve(lambda e: e.tensor_copy(out=xcb[:, 0:N], in_=xc[:, 0:N]), L(bT[0]), L(bxcb))
        yield 1
        if is_main:
            yield 1
        bkr = bankF.next()
        pe(lambda e, bkr=bkr: e.matmul(ps[:, bkr, 0:N], lhsT=Wa[:, j, :], rhs=xcb[:, 0:N], start=True, stop=True), L(bWa, bxcb), [bps[bkr]], c=N)
        bki = bankF.next()
        pe(lambda e, bki=bki: e.matmul(ps[:, bki, 0:N], lhsT=Wx[:, j, :], rhs=xcb[:, 0:N], start=True, stop=True), L(bWx, bxcb), [bps[bki]], c=N)
        er, ei, a_, a2 = T[1], T[2], T[3], T[4]
        act(lambda e, bkr=bkr: e.activation(out=er[:, 0:N], in_=ps[:, bkr, 0:N], func=AF.Exp, scale=-1.0, bias=nba[:, j:j + 1]), [bps[bkr], bconst], L(bT[1]))
        act(lambda e, bki=bki: e.activation(out=ei[:, 0:N], in_=ps[:, bki, 0:N], func=AF.Exp, scale=-1.0, bias=nbx[:, j:j + 1]), [bps[bki], bconst], L(bT[2]))
        yield 1
        act(lambda e: e.activation(out=er[:, 0:N], in_=er[:, 0:N], func=AF.Ln, bias=1.0), L(bT[1]), L(bT[1]))
        act(lambda e: e.activation(out=ei[:, 0:N], in_=ei[:, 0:N], func=AF.Ln, bias=1.0), L(bT[2]), L(bT[2]))
        yield 1
        act(lambda e: e.activation(out=er[:, 0:N], in_=er[:, 0:N], func=AF.Exp, scale=-1.0), L(bT[1]), L(bT[1]))
        yield 1
        act(lambda e: e.activation(out=a2[:, 0:N], in_=er[:, 0:N], func=AF.Exp, scale=c2[:, j:j + 1]), L(bT[1], bconst), L(bT[4]))
        act(lambda e: e.activation(out=a_[:, 0:N], in_=er[:, 0:N], func=AF.Exp, scale=cneg[:, j:j + 1]), L(bT[1], bconst), L(bT[3]))
        yield 1
        act(lambda e: e.activation(out=a2[:, 0:N], in_=a2[:, 0:N], func=AF.Ln, scale=-1.0, bias=1.0), L(bT[4]), L(bT[4]))
        yield 1
        dve(lambda e: e.scalar_tensor_tensor(out=a2[:, 0:N], in0=a2[:, 0:N], scalar=0.5, in1=ei[:, 0:N], op0=ALU.mult, op1=ALU.subtract), L(bT[4], bT[2]), L(bT[4]))
        yield 1
        act(lambda e: e.activation(out=a2[:, 0:N], in_=a2[:, 0:N], func=AF.Exp), L(bT[4]), L(bT[4]))
        yield 1
        pool(lambda e: e.tensor_tensor(out=xc[:, 0:N], in0=a2[:, 0:N], in1=xc[:, 0:N], op=ALU.mult), L(bT[4], bT[0]), L(bT[0]))
        if masked:
            pool(lambda e: e.tensor_tensor(out=xc[:, 0:N], in0=xc[:, 0:N], in1=tmp2[:, vmo:vmo + N], op=ALU.mult), L(bT[0], bvmh[vmo // G]), L(bT[0]))
        yield 1
        h_ = T[2]
        dve(lambda e: e.tensor_tensor_scan(out=h_[:, 0:N], data0=a_[:, 0:N], data1=xc[:, 0:N], initial=hstate[:, j:j + 1], op0=ALU.mult, op1=ALU.add), L(bT[3], bT[0], bhst[j]), L(bT[2]))
        dve(lambda e: e.tensor_copy(out=hstate[:, j:j + 1], in_=h_[:, N - 1:N]), L(bT[2]), [bhst[j]])
        yield 1
        if is_main:
            bky = proj_chunk(1280 + 128 * j, n, ntile, uTb, buTb)
            yr, y2 = T[4], T[1]
            act(lambda e, bky=bky: e.copy(out=yr[:, 0:N], in_=ps[:, bky, 0:N]), [bps[bky]], L(bT[4]))
            act(lambda e, bky=bky: e.activation(out=y2[:, 0:N], in_=ps[:, bky, 0:N], func=AF.Square), [bps[bky]], L(bT[1]))
            yield 1
            dve(lambda e: e.tensor_scalar(out=y2[:, 0:N], in0=y2[:, 0:N], scalar1=0.044715, scalar2=1.0, op0=ALU.mult, op1=ALU.add), L(bT[1]), L(bT[1]))
            dve(lambda e: e.tensor_tensor(out=y2[:, 0:N], in0=y2[:, 0:N], in1=yr[:, 0:N], op=ALU.mult), L(bT[1], bT[4]), L(bT[1]))
            yield 1
            act(lambda e: e.activation(out=y2[:, 0:N], in_=y2[:, 0:N], func=AF.Exp, scale=-2.0 * GELU_C), L(bT[1]), L(bT[1]))
            yield 1
            act(lambda e: e.activation(out=y2[:, 0:N], in_=y2[:, 0:N], func=AF.Ln, bias=1.0), L(bT[1]), L(bT[1]))
            yield 1
            act(lambda e: e.activation(out=y2[:, 0:N], in_=y2[:, 0:N], func=AF.Exp, scale=-1.0), L(bT[1]), L(bT[1]))
            yield 1
            pool(lambda e: e.tensor_tensor(out=yr[:, 0:N], in0=yr[:, 0:N], in1=y2[:, 0:N], op=ALU.mult), L(bT[4], bT[1]), L(bT[4]))
            yield 1
            dve(lambda e: e.tensor_tensor(out=recT[:, j, 0:N], in0=yr[:, 0:N], in1=h_[:, 0:N], op=ALU.mult), L(bT[4], bT[2]), [brecT[j]])
            yield 1

    def kv_evac(ntile, blk0, halo, uTb=None, buTb=None):
        uTb = uT if uTb is None else uTb
        buTb = buT if buTb is None else buTb
        n = ntile * 128
        bk = proj_chunk(512, n, ntile, uTb, buTb)
        kbufs = bK[blk0:blk0 + ntile]
        act(lambda e, bk=bk: e.copy(out=KA[0:64, blk0 * 128:blk0 * 128 + n], in_=ps[0:64, bk, 0:n]), [bps[bk]], kbufs)
        act(lambda e, bk=bk: e.copy(out=KB[64:128, blk0 * 128:blk0 * 128 + n], in_=ps[64:128, bk, 0:n]), [bps[bk]], kbufs)
        bv = bankF.next()
        for t in range(ntile):
            for k in range(8):
                pe(lambda e, k=k, t=t, bv=bv: e.matmul(ps[:, bv, t * 128:(t + 1) * 128], lhsT=uTb[:, k, t * 128:(t + 1) * 128], rhs=Win[:, k, 640:768], start=(k == 0), stop=(k == 7)),
                   [bWin[k], buTb[t]], [bps[bv]], c=128)
        for t in range(ntile):
            src = ps[:, bv, t * 128:(t + 1) * 128].rearrange("p (a b) -> p a b", a=2)
            if halo:
                dve(lambda e, t=t, src=src: e.tensor_scalar(out=Vaug[:, blk0 + t, :, 0:64], in0=src, scalar1=vcol_s[:, 0:1], scalar2=None, op0=ALU.mult), [bps[bv], bconst], [bV[blk0 + t]])
                dve(lambda e, t=t: e.tensor_scalar(out=Vaug[:, blk0 + t, :, 64:65], in0=Vaug[:, blk0 + t, :, 64:65], scalar1=vcol_s[:, 0:1], scalar2=None, op0=ALU.mult), [bV[blk0 + t], bconst], [bV[blk0 + t]])
            else:
                dve(lambda e, t=t, src=src: e.tensor_copy(out=Vaug[:, blk0 + t, :, 0:64], in_=src), [bps[bv]], [bV[blk0 + t]])

    def rr(*gens):
        gens = [g for g in gens if g is not None]
        while gens:
            for g in list(gens):
                try:
                    next(g)
                    yield 1
                except StopIteration:
                    gens.remove(g)

    def rr_w(ga, na, gb):
        da = False
        db = gb is None
        while not (da and db):
            for _ in range(na):
                if da:
                    break
                try:
                    next(ga)
                    yield 1
                except StopIteration:
                    da = True
            if not db:
                try:
                    next(gb)
                    yield 1
                except StopIteration:
                    db = True

    def skewed(factories, D):
        active = []
        nxt = 0
        r = 0
        while nxt < len(factories) or active:
            if r % D == 0 and nxt < len(factories):
                active.append(factories[nxt]())
                nxt += 1
            for g_ in list(active):
                try:
                    next(g_)
                except StopIteration:
                    active.remove(g_)
            r += 1
            yield 1

    def seq(*gens):
        for g in gens:
            if g is not None:
                yield from g

    uTbufs = [(uT, buT), (u2T, bu2T)]

    def prefix_m1(pg):
        ntile = 4 if pg < 8 else 1
        tok0 = pg * G
        uTb, buTb = uTbufs[pg % 2]
        for t in range(ntile):
            r0 = tok0 + t * 128
            S.add("sp", lambda e, t=t, r0=r0: e.dma_start(out=xt[t % 2][:], in_=xp.ap()[r0:r0 + 128, :]), (), [bxt[t % 2]], dma_key=f"d_x{t % 2}")
            norm_transpose(xt[t % 2][:], bxt[t % 2], None, None, uTb, buTb, t, False)
            yield 1

    def run_prefix(pgs):
        D = 3
        ntl = lambda pg: 4 if pg < 8 else 1

        def vm_load(gi):
            pg = pgs[gi]
            n = ntl(pg) * 128
            off = (gi % 2) * G
            S.add("sp", lambda e: e.dma_start(out=tmp2[:, off:off + n], in_=vrow.ap()[0:1, pg * G:pg * G + n].partition_broadcast(128)), (), [bvmh[gi % 2]], dma_key=f"d_vm{gi % 2}")

        facts = []
        for gi, pg in enumerate(pgs):
            for j in range(4):
                i = 4 * gi + j
                uTb, buTb = uTbufs[pg % 2]
                facts.append(lambda j=j, pg=pg, i=i, uTb=uTb, buTb=buTb, gi=gi: lru_chunk(j, ntl(pg) * 128, ntl(pg), False, True, tsets[i % 4], uTb, buTb, (gi % 2) * G))
        m1g = {gi: prefix_m1(pg) for gi, pg in enumerate(pgs)}
        vm_load(0)
        for _ in m1g[0]:
            pass
        active = []
        nxt = 0
        r = 0
        while nxt < len(facts) or active:
            gi_n = (r + 4) // (4 * D)
            if 1 <= gi_n < len(pgs):
                t_ = (r + 4) - gi_n * 4 * D
                if t_ == 0:
                    vm_load(gi_n)
                if 0 <= t_ < 4:
                    try:
                        next(m1g[gi_n])
                    except StopIteration:
                        pass
            if r % D == 0 and nxt < len(facts):
                active.append(facts[nxt]())
                nxt += 1
            for g_ in list(active):
                try:
                    next(g_)
                except StopIteration:
                    active.remove(g_)
            r += 1
        if pgs[-1] == 8:
            uTb, buTb = uTbufs[0]
            kv_evac(1, 0, True, uTb, buTb)

    def front_a(g):
        for t in range(4):
            r0 = g * G + t * 128
            S.add("sp", lambda e, t=t, r0=r0: e.dma_start(out=xt[t % 2][:], in_=xm.ap()[r0:r0 + 128, :]), (), [bxt[t % 2]], dma_key=f"d_x{t % 2}")
            norm_transpose(xt[t % 2][:], bxt[t % 2], None, None, uT, buT, t, False)
            yield 1
        for c in range(4):
            bk = proj_chunk(128 * c, G, 4)
            act(lambda e, bk=bk, c=c: e.copy(out=qT[:, c, :], in_=ps[:, bk, :]), [bps[bk]], [bqT[c]])
            yield 1
        if g > 0:
            dve(lambda e: e.tensor_copy(out=KA[0:64, 0:128], in_=KA[0:64, 512:640]), [bK[4]], [bK[0]])
            dve(lambda e: e.tensor_copy(out=KB[64:128, 0:128], in_=KB[64:128, 512:640]), [bK[4]], [bK[0]])
            dve(lambda e: e.tensor_copy(out=Vaug[:, 0, :, :], in_=Vaug[:, 4, :, :]), [bV[4]], [bV[0]])
        kv_evac(4, 1, False)
        yield 1

    def front_b(g):
        lru = skewed([lambda j=j: lru_chunk(j, G, 4, True, False, tsets[j % 2]) for j in range(4)], 10)
        import os as _os
        if _os.environ.get("LRU_SEQ") == "1":
            for j in range(4):
                yield from lru_chunk(j, G, 4, True, False, tsets[j % 2])
            yield from attention(g)
        else:
            yield from rr(lru, attention(g))

    def attention(g):
        for n in range(4):
            b_o = [3, 4]
            def s_part(c):
                bs = bankF.next()
                pt = PT[c % 2]
                bpt = bPT[c % 2]
                qs = qT[:, c, n * 128:(n + 1) * 128]
                for a, Kt in enumerate((KA, KB)):
                    for pc in range(2):
                        blk = n + pc
                        col = (a * 2 + pc) * 128
                        pe(lambda e, Kt=Kt, blk=blk, col=col, bs=bs, qs=qs: e.matmul(ps[:, bs, col:col + 128], lhsT=Kt[:, blk * 128:(blk + 1) * 128], rhs=qs, start=True, stop=True),
                           [bK[blk], bqT[c]], [bps[bs]], c=128)
                for a in range(2):
                    h = c + 4 * a
                    act(lambda e, a=a, h=h, bs=bs, pt=pt: e.activation(out=pt[:, a * 256:(a + 1) * 256], in_=ps[:, bs, a * 256:(a + 1) * 256], func=AF.Exp, scale=0.125, bias=nsink[:, h:h + 1]),
                        [bps[bs], bconst], [bpt])
                dve(lambda e, pt=pt: e.tensor_tensor(out=pt[:], in0=pt[:], in1=maskPT[:], op=ALU.mult), [bpt, bmask], [bpt])

            def pv_part(c):
                pt = PT[c % 2]
                bpt = bPT[c % 2]
                for a in range(2):
                    h = c + 4 * a
                    bo = b_o[h // 4]
                    o0 = (h % 4) * 65
                    for pc in range(2):
                        blk = n + pc
                        col = (a * 2 + pc) * 128
                        pe(lambda e, a=a, blk=blk, col=col, bo=bo, o0=o0, pt=pt, pc=pc: e.matmul(ps[:, bo, o0:o0 + 65], lhsT=pt[:, col:col + 128], rhs=Vaug[:, blk, a, :], start=(pc == 0), stop=(pc == 1)),
                           [bpt, bV[blk]], [bps[bo]], c=65)

            s_part(0)
            yield 1
            for c in range(4):
                if c + 1 < 4:
                    s_part(c + 1)
                    yield 1
                pv_part(c)
                yield 1
            at = atok[n % 2]
            bat = batok[n % 2]
            for hh in range(2):
                bo = b_o[hh]
                ov = ps[:, bo, 0:260].rearrange("p (h d) -> p h d", h=4)
                dve(lambda e, ov=ov, hh=hh: e.tensor_scalar(out=den[:, hh * 4:hh * 4 + 4], in0=ov[:, :, 64], scalar1=1.0, scalar2=None, op0=ALU.add), [bps[bo]], [bden])
                dve(lambda e, hh=hh: e.reciprocal(out=den[:, hh * 4:hh * 4 + 4], in_=den[:, hh * 4:hh * 4 + 4]), [bden], [bden])
                for hq in range(4):
                    h = hh * 4 + hq
                    dve(lambda e, ov=ov, hq=hq, h=h, at=at: e.tensor_scalar(out=at[:, h * 64:(h + 1) * 64], in0=ov[:, hq, 0:64], scalar1=den[:, h:h + 1], scalar2=None, op0=ALU.mult), [bps[bo], bden], [bat])
            yield 1
            yield 1
            bt = bankF.next()
            for k in range(4):
                pe(lambda e, k=k, bt=bt, at=at: e.transpose(out=psb[:, bt, k * 128:(k + 1) * 128], in_=at[:, k * 128:(k + 1) * 128], identity=ident[:]), [bat, bident], [bps[bt]], c=128)
            act(lambda e, bt=bt, n=n: e.copy(out=attnT[:, :, n * 128:(n + 1) * 128], in_=psb[:, bt, 0:512].rearrange("p (k t) -> p k t", k=4)), [bps[bt]], [battnT[n]])
            yield 1

    def wload(slot, src_ap, view):
        return S.add("pool", lambda e: e.dma_start(out=view, in_=src_ap), (), [bwst[slot]], dma_key=f"d_w{slot}")

    w1s = w_ff1.ap().rearrange("f p (a b) -> f p a b", b=1024)
    w2s = w_ff2.ap().rearrange("f p (a b) -> f p a b", b=1024)

    def slot_flat(s_):
        return wst[s_][:].rearrange("p (a b) -> p a b", b=1024)
    wov = w_out.ap().rearrange("(k p) n -> p k n", p=128)

    def slot_w1(s):
        return wst[s][:].rearrange("p (k n) -> p k n", k=8)

    def slot_w2(s):
        return wst[s][:].rearrange("p (m n) -> p m n", m=32)

    def back_m5(g):
        if g == 0:
            for hf in range(2):
                wload(hf, wov[:, :, hf * 512:(hf + 1) * 512], slot_w1(hf))
        for t in range(4):
            r0 = g * G + t * 128
            S.add("sp", lambda e, t=t, r0=r0: e.dma_start(out=h1[:, t, :], in_=xm.ap()[r0:r0 + 128, :]), (), [bh1[t]], dma_key=f"d_h{t}")
            for hf in range(2):
                bk = bankB.next()
                for k in range(8):
                    src = attnT[:, k, t * 128:(t + 1) * 128] if k < 4 else recT[:, k - 4, t * 128:(t + 1) * 128]
                    bsrc = battnT[t] if k < 4 else brecT[k - 4]
                    pe(lambda e, k=k, bk=bk, src=src, hf=hf: e.matmul(ps[:, bk, :], lhsT=src, rhs=slot_w1(hf)[:, k, :], start=(k == 0), stop=(k == 7)),
                       [bsrc, bwst[hf]], [bps[bk]])
                act(lambda e, bk=bk, t=t, hf=hf: e.copy(out=ftok[:, t, hf * 512:(hf + 1) * 512], in_=ps[:, bk, :]), [bps[bk]], [bftok[t]])
            yield 8200
        wload(0, w1s[0], slot_flat(0))
        wload(1, w1s[1], slot_flat(1))
        state["allow_fb"] = True
        for t in range(4):
            col = statB.next()
            act(lambda e, t=t, col=col: e.activation(out=ub[2][:], in_=ftok[:, t, :], func=AF.Square, scale=1.0 / 32.0, accum_out=st_ms[:, col:col + 1]), [bftok[t]], [bub[2], bst[col]])
            rstd_from_ms(col)
            dve(lambda e, t=t, col=col: e.scalar_tensor_tensor(out=ftok[:, t, :], in0=ftok[:, t, :], scalar=st_rs[:, col:col + 1], in1=gt2[:], op0=ALU.mult, op1=ALU.mult), [bftok[t], bst[col], bgt2], [bftok[t]])
            dve(lambda e, t=t: e.tensor_tensor(out=h1[:, t, :], in0=h1[:, t, :], in1=ftok[:, t, :], op=ALU.add), [bh1[t], bftok[t]], [bh1[t]])
            norm_transpose(h1[:, t, :], bh1[t], gt3, bgt3, u2T, bu2T, t, True)
            yield 12300

    def back_ffn(g):
        for fg in range(8):
            s = fg % 2
            for m in range(4):
                bk = bankB.next()
                mm = fg * 4 + m
                for k in range(8):
                    pe(lambda e, k=k, bk=bk, s=s, m=m: e.matmul(ps[:, bk, :], lhsT=slot_w1(s)[:, k, m * 128:(m + 1) * 128], rhs=u2T[:, k, :], start=(k == 0), stop=(k == 7)),
                       [bwst[s]] + bu2T, [bps[bk]])
                act(lambda e, bk=bk, mm=mm: e.activation(out=f1T[:, mm, :], in_=ps[:, bk, :], func=AF.Relu), [bps[bk]], [bf1T[mm]])
                dve(lambda e, mm=mm: e.tensor_tensor(out=f1T[:, mm, :], in0=f1T[:, mm, :], in1=f1T[:, mm, :], op=ALU.mult), [bf1T[mm]], [bf1T[mm]])
                yield 900
            nxt = fg + 2
            if nxt < 8:
                wload(s, w1s[nxt], slot_flat(s))
            else:
                oc = nxt - 8
                wload(s, w2s[oc], slot_flat(s))
        for oc in range(8):
            s = oc % 2
            bk = bankB.next()
            for m in range(32):
                pe(lambda e, m=m, bk=bk, s=s: e.matmul(ps[:, bk, :], lhsT=slot_w2(s)[:, m, :], rhs=f1T[:, m, :], start=(m == 0), stop=(m == 31)),
                   [bwst[s], bf1T[m]], [bps[bk]])
                if m % 8 == 7:
                    yield 900
            if oc + 2 < 8:
                wload(s, w2s[oc + 2], slot_flat(s))
            act(lambda e, bk=bk: e.copy(out=fTc, in_=ps[:, bk, :]), [bps[bk]], [bfTc])
            bt = bankB.next()
            for t in range(4):
                pe(lambda e, t=t, bt=bt: e.transpose(out=ps[:, bt, t * 128:(t + 1) * 128], in_=tmp2[:, t * 128:(t + 1) * 128], identity=ident_f[:]), [bfTc, bident], [bps[bt]], c=128)
            dve(lambda e, bt=bt, oc=oc: e.tensor_copy(out=ftok[:, :, oc * 128:(oc + 1) * 128], in_=ps[:, bt, :].rearrange("p (t f) -> p t f", t=4)), [bps[bt]], bftok)
            yield 900
        if g + 1 < n_main_groups:
            for hf in range(2):
                wload(hf, wov[:, :, hf * 512:(hf + 1) * 512], slot_w1(hf))

    def back_final(g):
        for t in range(4):
            col = statB.next()
            act(lambda e, t=t, col=col: e.activation(out=ub[2][:], in_=ftok[:, t, :], func=AF.Square, scale=1.0 / 32.0, accum_out=st_ms[:, col:col + 1]), [bftok[t]], [bub[2], bst[col]])
            rstd_from_ms(col)
            dve(lambda e, t=t, col=col: e.scalar_tensor_tensor(out=ftok[:, t, :], in0=ftok[:, t, :], scalar=st_rs[:, col:col + 1], in1=gt4[:], op0=ALU.mult, op1=ALU.mult), [bftok[t], bst[col], bgt4], [bftok[t]])
            dve(lambda e, t=t: e.tensor_tensor(out=h1[:, t, :], in0=h1[:, t, :], in1=ftok[:, t, :], op=ALU.add), [bh1[t], bftok[t]], [bh1[t]])
            r0 = g * G + t * 128
            out_ops.append(S.add("sp", lambda e, t=t, r0=r0: e.dma_start(out=out_d.ap()[r0:r0 + 128, :], in_=h1[:, t, :]), [bh1[t]], (), dma_key=f"d_o{t}"))
            yield 1200

    def run_alone(gen, who="A"):
        state["cur"] = who
        for _ in gen:
            pass

    def merge(genA, totA, genB, totB):
        a0, b0 = cost["A"], cost["B"]
        doneA = genA is None
        doneB = genB is None
        while not (doneA and doneB):
            pa = (cost["A"] - a0) / totA
            pb = (cost["B"] - b0) / totB
            if (not doneA) and (doneB or pa <= pb):
                state["cur"] = "A"
                try:
                    next(genA)
                except StopIteration:
                    doneA = True
            else:
                state["cur"] = "B"
                try:
                    next(genB)
                except StopIteration:
                    doneB = True

    def zip4(ga, gb):
        if ga is not None:
            for _ in range(4):
                yield next(ga)
                yield next(gb)
            for v in ga:
                yield v
        yield from gb

    def merge2(A_gens, genB):
        ai = 0
        budget = 0.0
        state["allow_fb"] = False
        while True:
            state["cur"] = "B"
            try:
                budget += next(genB)
            except StopIteration:
                break
            state["cur"] = "A"
            while budget > 0 and ai < len(A_gens):
                if ai >= 1 and not state["allow_fb"]:
                    break
                c0 = cost["A"]
                try:
                    next(A_gens[ai])
                except StopIteration:
                    ai += 1
                    continue
                budget -= max(cost["A"] - c0, 700.0)
        state["cur"] = "A"
        while ai < len(A_gens):
            for _ in A_gens[ai]:
                pass
            ai += 1

    setup()
    pgs = list(range(9 - n_pre_groups, 9))
    state["cur"] = "A"
    run_prefix(pgs)
    if pipelined:
        run_alone(front_a(0))
        run_alone(front_b(0))
        for g in range(n_main_groups):
            nxt = g + 1 < n_main_groups
            Bs = seq(zip4(back_final(g - 1) if g > 0 else None, back_m5(g)), back_ffn(g))
            merge2([front_a(g + 1), front_b(g + 1)] if nxt else [], Bs)
        run_alone(back_final(n_main_groups - 1), "B")
    else:
        for g in range(n_main_groups):
            run_alone(front_a(g))
            run_alone(front_b(g))
            run_alone(back_m5(g), "B")
            run_alone(back_ffn(g), "B")
            run_alone(back_final(g), "B")
    S.add("sp", None, extra_deps=out_ops)
    print("sbuf bytes remaining", nc.sbuf_bytes_remaining)
    S.emit(nc)
    es.close()
    return nc


_NC_CACHE = {}


def _prep_inputs(x, meta_tokens, g_pre_mix, w_in, conv_w, conv_b, w_a, b_a, w_x, b_x,
                 lru_lambda, attn_sinks, w_out, g_post_mix, g_pre_ffn, w_ff1, w_ff2, g_post_ffn):
    f = np.float32
    x = np.asarray(x, f)
    meta = np.asarray(meta_tokens, f)
    w_in0 = np.asarray(w_in, f)[0]
    order = []
    for c in range(4):
        order += list(range(c * 64, c * 64 + 64)) + list(range((4 + c) * 64, (4 + c) * 64 + 64))
    perm = np.array(order + list(range(512, 1792)))
    w_in_p = np.ascontiguousarray(w_in0[:, perm])

    def bd(w):
        w = np.asarray(w, f)[0]
        o = np.zeros((4, 128, 128), f)
        for j in range(4):
            o[j, 0:64, 0:64] = w[2 * j]
            o[j, 64:128, 64:128] = w[2 * j + 1]
        return o

    def chan(v):
        return np.ascontiguousarray(np.asarray(v, f).reshape(4, 128).T)

    cw = np.asarray(conv_w, f)[0]
    cw_l = np.ascontiguousarray(cw.reshape(4, 4, 128).transpose(2, 1, 0).reshape(128, 16))
    shared = {
        "w_in": w_in_p,
        "w_out": np.ascontiguousarray(np.asarray(w_out, f)[0]),
        "w_ff1": np.ascontiguousarray(np.asarray(w_ff1, f)[0].reshape(8, 128, 8, 512).transpose(2, 1, 0, 3).reshape(8, 128, 4096)),
        "w_ff2": np.ascontiguousarray(np.asarray(w_ff2, f)[0].reshape(32, 128, 8, 128).transpose(2, 1, 0, 3).reshape(8, 128, 4096)),
        "wa_bd": bd(w_a),
        "wx_bd": bd(w_x),
        "cw": cw_l,
        "cb": chan(np.asarray(conv_b, f)[0]),
        "ba": chan(np.asarray(b_a, f)[0]),
        "bx": chan(np.asarray(b_x, f)[0]),
        "lam": chan(np.asarray(lru_lambda, f)[0]),
        "sinks": np.ascontiguousarray(np.asarray(attn_sinks, f)[0].reshape(1, 8)),
        "g1c": np.ascontiguousarray(np.asarray(g_pre_mix, f)[0].reshape(8, 128).T),
        "g2": np.ascontiguousarray(np.asarray(g_post_mix, f)[0].reshape(1, D)),
        "g3": np.ascontiguousarray(np.asarray(g_pre_ffn, f)[0].reshape(1, D)),
        "g4": np.ascontiguousarray(np.asarray(g_post_ffn, f)[0].reshape(1, D)),
    }
    in_maps = []
    for c in range(8):
        b, hf = c // 2, c % 2
        xp = np.zeros((NPRE, D), f)
        vrow = np.zeros((1, NPRE), f)
        vcol = np.zeros((128, 1), f)
        if hf == 0:
            xp[NPRE - 16:] = meta
            vrow[0, NPRE - 16:] = 1.0
            vcol[112:, 0] = 1.0
        else:
            xp[112:128] = meta
            xp[128:] = x[b, 0:4096]
            vrow[0, 112:] = 1.0
            vcol[:, 0] = 1.0
        m = dict(shared)
        m["xp"] = xp
        m["xm"] = np.ascontiguousarray(x[b, hf * 4096:(hf + 1) * 4096])
        m["vrow"] = vrow
        m["vcol"] = vcol
        in_maps.append(m)
    return in_maps


def kernel(**inputs):
    in_maps = _prep_inputs(**inputs)
    if "nc" not in _NC_CACHE:
        _NC_CACHE["nc"] = build_program()
    nc = _NC_CACHE["nc"]
    res = run_bass_kernel_spmd(nc, in_maps, core_ids=list(range(8)))
    out = np.empty((4, 8192, D), np.float32)
    for c in range(8):
        b, hf = c // 2, c % 2
        out[b, hf * 4096:(hf + 1) * 4096] = res.results[c]["out"]
    return out
```
